# Optimizing a Trainium2 kernel written in Bass

```python
import jax, jax.numpy as jnp
from jax import lax
import numpy as np

D_MODEL = 1024
BATCH = 8
SEQ = 2048
DEPTH = 1
DEC_BATCH = 32
DEC_SEQ = 8
PAST_LEN = 16384
PAGE_SIZE = 128

CHUNK = 128
A_WIDTH = D_MODEL
A_GROUPS = 8
A_GROUP_DIM = A_WIDTH // A_GROUPS
HEAD_DIM = 64
B_SLOTS = 8
DILATED_PAIRS = ((128, 1), (512, 4), (2048, 16))
N_GROUPS_B = len(DILATED_PAIRS)
B_QKV = N_GROUPS_B * B_SLOTS * HEAD_DIM
B_OUT = B_SLOTS * HEAD_DIM
N_BRANCH = 2
IN_SIZES = (A_WIDTH, A_WIDTH, A_WIDTH, B_QKV, B_QKV, B_QKV, B_OUT, N_BRANCH * D_MODEL)
D_IN = sum(IN_SIZES)
SPLIT_IDX = tuple(int(i) for i in np.cumsum(IN_SIZES)[:-1])
EPS = 1e-6
NEG = -1e30

kernel_name = "hybrid_gmlp_dilated_attn_step"


def _rmsnorm(x, g):
    x32 = x.astype(jnp.float32)
    y = x32 * lax.rsqrt(jnp.mean(x32 * x32, axis=-1, keepdims=True) + EPS)
    return (y * g.astype(jnp.float32)).astype(x.dtype)


def _layernorm(x, g, b):
    x32 = x.astype(jnp.float32)
    mu = jnp.mean(x32, axis=-1, keepdims=True)
    var = jnp.mean(jnp.square(x32 - mu), axis=-1, keepdims=True)
    y = (x32 - mu) * lax.rsqrt(var + EPS)
    return (y * g.astype(jnp.float32) + b.astype(jnp.float32)).astype(x.dtype)


def _dilated_prompt(q, k, v, window, dilation):
    b, s, h, dh = q.shape
    n_back = window // dilation
    blk = n_back
    L = s // dilation
    nb = -(-L // blk)
    lp = nb * blk

    def to_phase(a):
        return a.reshape(b, L, dilation, h, dh).transpose(0, 2, 1, 3, 4)

    qp = jnp.pad(to_phase(q), ((0, 0), (0, 0), (0, lp - L), (0, 0), (0, 0)))
    pad_kv = ((0, 0), (0, 0), (blk, lp - L), (0, 0), (0, 0))
    kp = jnp.pad(to_phase(k), pad_kv)
    vp = jnp.pad(to_phase(v), pad_kv)

    def two_blocks(a):
        prev = a[:, :, :lp].reshape(b, dilation, nb, blk, h, dh)
        cur = a[:, :, blk:].reshape(b, dilation, nb, blk, h, dh)
        return jnp.concatenate([prev, cur], axis=3)

    kb, vb = two_blocks(kp), two_blocks(vp)
    qb = qp.reshape(b, dilation, nb, blk, h, dh)
    scores = jnp.einsum("bdnqhc,bdnkhc->bdnhqk", qb, kb).astype(jnp.float32) * (HEAD_DIM ** -0.5)
    qi = np.arange(lp).reshape(nb, blk, 1)
    ki = (np.arange(nb)[:, None, None] - 1) * blk + np.arange(2 * blk)[None, None, :]
    dist = qi - ki
    mask = (dist >= 0) & (dist <= n_back) & (ki >= 0)
    scores = jnp.where(mask[None, None, :, None], scores, NEG)
    m = jnp.max(scores, axis=-1)
    p = jnp.exp(scores - m[..., None])
    l = jnp.sum(p, axis=-1)
    acc = jnp.einsum("bdnhqk,bdnkhc->bdnqhc", p, vb.astype(jnp.float32))
    acc = acc.reshape(b, dilation, lp, h, dh)[:, :, :L].transpose(0, 2, 1, 3, 4).reshape(b, s, h, dh)

    def stat_back(a):
        a = a.transpose(0, 1, 2, 4, 3).reshape(b, dilation, lp, h)[:, :, :L]
        return a.transpose(0, 2, 1, 3).reshape(b, s, h)

    return acc, stat_back(m), stat_back(l)


def _dilated_sample(q, k, v, kv_cache, window, dilation):
    lw = kv_cache.shape[1]
    t = q.shape[1]
    n_keys = window // dilation + 1
    k_all = jnp.concatenate([kv_cache[:, :, 0], k], axis=1)
    v_all = jnp.concatenate([kv_cache[:, :, 1], v], axis=1)
    idx = lw + np.arange(t)[:, None] - dilation * np.arange(n_keys)[None, :]
    valid = idx >= 0
    idx = np.maximum(idx, 0)
    kg = k_all[:, idx]
    vg = v_all[:, idx]
    scores = jnp.einsum("bthc,btkhc->bthk", q, kg).astype(jnp.float32) * (HEAD_DIM ** -0.5)
    scores = jnp.where(valid[None, :, None, :], scores, NEG)
    m = jnp.max(scores, axis=-1)
    p = jnp.exp(scores - m[..., None])
    l = jnp.sum(p, axis=-1)
    acc = jnp.einsum("bthk,btkhc->bthc", p, vg.astype(jnp.float32))
    return acc, m, l


def _layer(x, c, kv_caches, w_cond, b_cond, g_pre, w_in, ln_v_g, ln_v_b,
           w_spatial, b_spatial, w_proj_a, w_proj_b, w_out, g_post):
    bsz, s, _ = x.shape
    mod = jax.nn.silu(c) @ w_cond + b_cond
    shift, scale, gate = jnp.split(mod, 3, axis=-1)
    h = _rmsnorm(x, g_pre) * (1 + scale[:, None, :]) + shift[:, None, :]
    proj = h @ w_in
    u_a, v_a, z_a, q, k, v, z_b, gate_logits = jnp.split(proj, SPLIT_IDX, axis=-1)

    v_n = _layernorm(v_a, ln_v_g, ln_v_b)
    causal = np.tril(np.ones((CHUNK, CHUNK), dtype=bool))
    w_sp = jnp.where(causal[None], w_spatial, 0)
    if kv_caches is None:
        vg = v_n.reshape(bsz, s // CHUNK, CHUNK, A_GROUPS, A_GROUP_DIM)
        zs = jnp.einsum("gts,bnsgc->bntgc", w_sp, vg) + b_spatial.T[None, None, :, :, None]
    else:
        vg = v_n.reshape(bsz, s, A_GROUPS, A_GROUP_DIM)
        zs = jnp.einsum("gts,bsgc->btgc", w_sp[:, :s, :s], vg) + b_spatial[:, :s].T[None, :, :, None]
    y_a = u_a * zs.reshape(bsz, s, A_WIDTH) * jax.nn.silu(z_a)

    q = q.reshape(bsz, s, N_GROUPS_B, B_SLOTS, HEAD_DIM)
    k = k.reshape(bsz, s, N_GROUPS_B, B_SLOTS, HEAD_DIM)
    v = v.reshape(bsz, s, N_GROUPS_B, B_SLOTS, HEAD_DIM)
    accs, ms, ls, kv_rows = [], [], [], []
    for gi, (window, dilation) in enumerate(DILATED_PAIRS):
        qg, kg, vg_ = q[:, :, gi], k[:, :, gi], v[:, :, gi]
        if kv_caches is None:
            acc, m, l = _dilated_prompt(qg, kg, vg_, window, dilation)
            kv_rows.append(jnp.stack([kg, vg_], axis=2)[:, s - min(window, s):])
        else:
            acc, m, l = _dilated_sample(qg, kg, vg_, kv_caches[gi], window, dilation)
            kv_rows.append(jnp.stack([kg, vg_], axis=2))
        accs.append(acc)
        ms.append(m)
        ls.append(l)
    ms = jnp.stack(ms)
    m_all = jnp.max(ms, axis=0)
    wts = jnp.exp(ms - m_all)
    den = jnp.sum(wts * jnp.stack(ls), axis=0)
    attn = jnp.sum(wts[..., None] * jnp.stack(accs), axis=0) / den[..., None]
    y_b = attn.reshape(bsz, s, B_OUT).astype(x.dtype) * jax.nn.silu(z_b)

    g_a, g_b = jnp.split(jax.nn.sigmoid(gate_logits), 2, axis=-1)
    merged = g_a * (y_a @ w_proj_a) + g_b * (y_b @ w_proj_b)
    out = merged @ w_out
    x_new = x + gate[:, None, :] * _rmsnorm(out, g_post)
    return x_new, kv_rows, v_n


def setup_inputs(seed: int = 0) -> dict:
    key = jax.random.key(seed)
    ks = jax.random.split(key, 24)
    f32 = jnp.float32

    def nrm(k_, shape, scale=1.0):
        return jax.random.normal(k_, shape, f32) * scale

    def cache_shape(window):
        return (DEPTH, DEC_BATCH, min(window, PAST_LEN), 2, B_SLOTS, HEAD_DIM)

    return {
        "x_prompt": nrm(ks[0], (BATCH, SEQ, D_MODEL)),
        "x_sample": nrm(ks[1], (DEC_BATCH, DEC_SEQ, D_MODEL)),
        "cache_kv_w128": nrm(ks[2], cache_shape(DILATED_PAIRS[0][0])),
        "cache_kv_w512": nrm(ks[3], cache_shape(DILATED_PAIRS[1][0])),
        "cache_kv_w2048": nrm(ks[4], cache_shape(DILATED_PAIRS[2][0])),
        "c_prompt": nrm(ks[5], (BATCH, D_MODEL)),
        "c_sample": nrm(ks[6], (DEC_BATCH, D_MODEL)),
        "w_cond": nrm(ks[7], (DEPTH, D_MODEL, 3 * D_MODEL), D_MODEL ** -0.5),
        "b_cond": nrm(ks[8], (DEPTH, 3 * D_MODEL), 0.02),
        "g_pre": 1.0 + nrm(ks[9], (DEPTH, D_MODEL), 0.02),
        "w_in": nrm(ks[10], (DEPTH, D_MODEL, D_IN), D_MODEL ** -0.5),
        "ln_v_g": 1.0 + nrm(ks[11], (DEPTH, A_WIDTH), 0.02),
        "ln_v_b": nrm(ks[12], (DEPTH, A_WIDTH), 0.02),
        "w_spatial": nrm(ks[13], (DEPTH, A_GROUPS, CHUNK, CHUNK), CHUNK ** -0.5),
        "b_spatial": 1.0 + nrm(ks[14], (DEPTH, A_GROUPS, CHUNK), 0.02),
        "w_proj_a": nrm(ks[15], (DEPTH, A_WIDTH, D_MODEL), A_WIDTH ** -0.5),
        "w_proj_b": nrm(ks[16], (DEPTH, B_OUT, D_MODEL), B_OUT ** -0.5),
        "w_out": nrm(ks[17], (DEPTH, D_MODEL, D_MODEL), D_MODEL ** -0.5),
        "g_post": 1.0 + nrm(ks[18], (DEPTH, D_MODEL), 0.02),
    }


def reference(x_prompt, x_sample, cache_kv_w128, cache_kv_w512, cache_kv_w2048, c_prompt, c_sample,
              w_cond, b_cond, g_pre, w_in, ln_v_g, ln_v_b, w_spatial, b_spatial,
              w_proj_a, w_proj_b, w_out, g_post):
    caches = (cache_kv_w128, cache_kv_w512, cache_kv_w2048)
    y_p, y_s = x_prompt, x_sample
    kv_p = [[] for _ in DILATED_PAIRS]
    kv_s = [[] for _ in DILATED_PAIRS]
    v_rows = []
    for layer in range(DEPTH):
        params = (w_cond[layer], b_cond[layer], g_pre[layer], w_in[layer], ln_v_g[layer], ln_v_b[layer],
                  w_spatial[layer], b_spatial[layer], w_proj_a[layer], w_proj_b[layer], w_out[layer],
                  g_post[layer])
        y_p, rows_p, _ = _layer(y_p, c_prompt, None, *params)
        layer_caches = (caches[0][layer], caches[1][layer], caches[2][layer])
        y_s, rows_s, v_n_s = _layer(y_s, c_sample, layer_caches, *params)
        for gi in range(N_GROUPS_B):
            kv_p[gi].append(rows_p[gi])
            kv_s[gi].append(rows_s[gi])
        v_rows.append(v_n_s)
    return (y_p, y_s,
            jnp.stack(kv_p[0]), jnp.stack(kv_p[1]), jnp.stack(kv_p[2]),
            jnp.stack(kv_s[0]), jnp.stack(kv_s[1]), jnp.stack(kv_s[2]),
            jnp.stack(v_rows))
```

```python
import contextlib
import numpy as np
import concourse.bass as bass
import concourse.mybir as mybir
from concourse.bass_utils import run_bass_kernel_spmd

F32 = mybir.dt.float32
BF16 = mybir.dt.bfloat16
AF = mybir.ActivationFunctionType
ALU = mybir.AluOpType

D = 1024
S = 2048
NS = 32
T = S + NS
EPS = 1e-6
PAIRS = ((128, 1), (512, 4), (2048, 16))
CH = [(0, 512), (512, 512), (1024, 512), (1536, 512), (2048, 32)]
COMPUTE = ("pe", "act", "dve", "pool")


RXKEYS = set([f"vn{i}" for i in range(17)] + ["bc_b", "bc_gpre", "bc_gpost", "MA", "MAs", "xt0", "xt1", "xt2", "t10", "t11", "hb0", "hb1", "junk0", "junk1", "c_sb", "c_th",
              "c_bf", "cTp", "cTs", "kstg0", "kstg1", "kstg2", "kstg3",
              "qT0", "qT1", "kT0", "kT1", "Vaug0", "Vaug1", "PT0", "PT1", "PT2", "Kc0", "Kc1", "Kc2", "Kc3", "Vc0", "Vc1", "Vc2", "Vc3",
              "KTs0", "KTs1", "KTs2", "KTs3", "PTn", "PTc0", "PTc1", "PTc2", "PTc3", "vstg0", "vstg1", "vstg2", "vstg3", "ztmp", "rscr",
              "vn", "lng", "lnb", "va0", "va1", "va2", "vb2", "bsp", "hbsp", "wspT", "wspTs", "wspn", "wspb", "wsps", "wspsb",
              "mgT", "x40", "x41", "o40", "o41", "sq4"]
             + [f"tA{i}{j}" for i in range(2) for j in range(5)] + [f"tM{i}{j}" for i in range(2) for j in range(6)])
RYKEYS = set(["ACC0", "ACC1", "SZB0", "SZB1", "RD", "yaT"] + [f"kstgA{i}" for i in range(6)])

class Prog:
    def __init__(self, nc):
        self.nc = nc
        self.ops = {e: [] for e in ("pe", "act", "dve", "pool", "sp")}
        self.cnt = {e: 0 for e in COMPUTE}
        self.seen = {e: {} for e in self.ops}
        self.last_w = {}
        self.readers = {}
        self.dma_cnt = {}
        self.final = {}
        self.alias = {}

    def _expand(self, reads, writes):
        reads = list(reads)
        writes = list(writes)
        for big in ("hT", "vn"):
            if big in reads:
                reads = [k for k in reads if k != big] + [f"{big}{i}" for i in range(17)]
        extra = []
        for k in reads + writes:
            a = "RX_ep" if k in RXKEYS else ("RY_ep" if k in RYKEYS else None)
            if a is not None and a not in extra:
                extra.append(a)
        writes = writes + [k for k in reads if k.startswith("ps") and k not in writes]
        return reads + extra, writes

    def _deps(self, eng, reads, writes):
        need = {}

        def add(tok, kind, key=None):
            sk, val, peng = tok
            if peng == eng and eng == "pe" and key != "pe_ser":
                return
            if need.get(sk, 0) < val:
                need[sk] = val

        for k in reads:
            w = self.last_w.get(k)
            if w is not None:
                add(w, "raw", k)
        for k in writes:
            w = self.last_w.get(k)
            if w is not None:
                add(w, "waw", k)
            for (sk, peng), val in self.readers.get(k, {}).items():
                add((sk, val, peng), "war")
        out = []
        for sk, val in need.items():
            if self.seen[eng].get(sk, 0) >= val:
                continue
            self.seen[eng][sk] = val
            out.append((sk, val))
        return out

    def _commit(self, tok, reads, writes):
        for k in writes:
            self.last_w[k] = tok
            self.readers[k] = {}
        sk, val, peng = tok
        for k in reads:
            d = self.readers.setdefault(k, {})
            if d.get((sk, peng), 0) < val:
                d[(sk, peng)] = val

    def op(self, eng, fn, reads=(), writes=()):
        reads, writes = self._expand(reads, writes)
        waits = self._deps(eng, reads, writes)
        self.cnt[eng] += 1
        tok = (eng, self.cnt[eng], eng)
        self.ops[eng].append((fn, waits, ("eng", eng)))
        self._commit(tok, reads, writes)
        return tok

    def dma(self, eng, sem, fn, reads=(), writes=(), final=False):
        reads, writes = self._expand(reads, writes)
        waits = self._deps(eng, reads, writes)
        self.dma_cnt[sem] = self.dma_cnt.get(sem, 0) + 16
        tok = ("dma:" + sem, self.dma_cnt[sem], "dma")
        self.ops[eng].append((fn, waits, ("dma", sem)))
        self._commit(tok, reads, writes)
        if final:
            self.final["dma:" + sem] = self.dma_cnt[sem]
        return tok

    def dma_group(self, eng, sem, fns, reads=(), writes=(), final=False):
        reads, writes = self._expand(reads, writes)
        waits = self._deps(eng, reads, writes)
        for j, fn in enumerate(fns):
            self.dma_cnt[sem] = self.dma_cnt.get(sem, 0) + 16
            self.ops[eng].append((fn, waits if j == 0 else [], ("dma", sem)))
        tok = ("dma:" + sem, self.dma_cnt[sem], "dma")
        self._commit(tok, reads, writes)
        if final:
            self.final["dma:" + sem] = self.dma_cnt[sem]
        return tok

    def emit(self):
        nc = self.nc
        targets = {e: set() for e in COMPUTE}
        for engname, lst in self.ops.items():
            for fn, waits, kind in lst:
                for sk, val in waits:
                    if sk in targets:
                        targets[sk].add(val)
        for e in COMPUTE:
            if self.cnt[e]:
                targets[e].add(self.cnt[e])
        rank = {}
        for e in COMPUTE:
            for r, idx in enumerate(sorted(targets[e])):
                rank[(e, idx)] = r + 1
        with contextlib.ExitStack() as st:
            sems = {}
            for e in COMPUTE:
                sems[e] = st.enter_context(nc.semaphore("s_" + e))
            for name in self.dma_cnt:
                sems["dma:" + name] = st.enter_context(nc.semaphore("d_" + name))
            block = st.enter_context(nc.Block())
            prog = self

            def run(engname, eng):
                idx = 0
                for fn, waits, kind in prog.ops[engname]:
                    for sk, val in waits:
                        if sk in targets:
                            eng.wait_ge(sems[sk], rank[(sk, val)])
                        else:
                            eng.wait_ge(sems[sk], val)
                    ins = fn(eng)
                    if kind[0] == "eng":
                        idx += 1
                        if (kind[1], idx) in rank:
                            ins.then_inc(sems[kind[1]], 1)
                    else:
                        ins.then_inc(sems["dma:" + kind[1]], 16)

            @block.tensor
            def _(eng):
                run("pe", eng)

            @block.scalar
            def _(eng):
                run("act", eng)

            @block.vector
            def _(eng):
                run("dve", eng)

            @block.gpsimd
            def _(eng):
                run("pool", eng)

            @block.sync
            def _(eng):
                run("sp", eng)
                for sk, val in prog.final.items():
                    eng.wait_ge(sems[sk], val)
                for e in COMPUTE:
                    if prog.cnt[e]:
                        eng.wait_ge(sems[e], rank[(e, prog.cnt[e])])


class Region:
    def __init__(self, tensor, nelem, name):
        self.t = tensor
        self.n = nelem
        self.name = name
        self.off = 0

    def reset(self):
        self.off = 0

    def alloc(self, free_shape, dt):
        n = int(np.prod(free_shape))
        nb = n * (2 if dt == F32 else 1)
        nb = (nb + 1) // 2 * 2
        assert self.off + nb <= self.n, (self.name, self.off, nb, self.n)
        v = self.t[:, self.off:self.off + nb]
        self.off += nb
        if dt == F32:
            v = v.bitcast(F32)
        if len(free_shape) == 2:
            v = v.rearrange("p (a b) -> p a b", b=free_shape[1])
        elif len(free_shape) == 3:
            v = v.rearrange("p (a b c) -> p a b c", b=free_shape[1], c=free_shape[2])
        return v


def build_nc(debug=False):
    nc = bass.Bass("TRN2", target_bir_lowering=False)
    din = lambda n, s: nc.dram_tensor(n, s, F32, kind="ExternalInput").ap()
    dout = lambda n, s: nc.dram_tensor(n, s, F32, kind="ExternalOutput").ap()
    xp = din("xp", [S, D])
    xs = din("xs", [NS, D])
    cc = din("cc", [5, D])
    ck = [din("ck0", [4, 128, 2, 512]), din("ck1", [4, 512, 2, 512]), din("ck2", [4, 2048, 2, 512])]
    w_cond = din("w_cond", [D, 3 * D])
    b_cond = din("b_cond", [3 * D])
    g_pre = din("g_pre", [D])
    w_in = din("w_in", [D, 10240])
    ln_g = din("ln_v_g", [D])
    ln_b = din("ln_v_b", [D])
    w_sp = din("w_spatial", [8, 128, 128])
    b_sp = din("b_spatial", [8, 128])
    w_pa = din("w_proj_a", [D, D])
    w_pb = din("w_proj_b", [512, D])
    w_out = din("w_out", [D, D])
    g_post = din("g_post", [D])
    yp = dout("yp", [S, D])
    ys = dout("ys", [NS, D])
    kvp = [dout("kvp0", [128, 2, 512]), dout("kvp1", [512, 2, 512]), dout("kvp2", [2048, 2, 512])]
    kvs = [dout("kvs0", [NS, 2, 512]), dout("kvs1", [NS, 2, 512]), dout("kvs2", [NS, 2, 512])]
    vch = dout("vch", [NS, D])

    st = contextlib.ExitStack()
    with st:
        sb = lambda n, s, d: st.enter_context(nc.sbuf_tensor(n, s, d))
        hT = sb("hT", [128, 8, T], BF16)
        ybT = sb("ybT", [128, 4, T], BF16)
        GGp = sb("GGp", [128, D], F32)
        GGs = sb("GGs", [128, D], F32)
        slabs = [sb(f"slab{i}", [128, 8, 512], BF16) for i in range(4)]
        RXN = 41 * 1024
        RYN = 8 * T
        RXt = sb("RX", [128, RXN], BF16)
        RYt = sb("RY", [128, RYN], BF16)
        ident = sb("ident", [128, 128], BF16)
        MKN = sb("MKN", [128, 3, 128], BF16)
        MC = sb("MC", [128, 8], BF16)
        MN = sb("MN", [32, 3, 32], BF16)
        tril = sb("tril", [128, 128], F32)
        ss = sb("ss", [128, 128], F32)
        st2 = sb("st2", [128, 128], F32)
        nhalf = sb("nhalf", [128, 1], F32)
        bnst = [sb(f"bnst{i}", [128, 12], F32) for i in range(3)]
        ps = [st.enter_context(nc.psum_tensor(f"ps{i}", [128, 512], F32)) for i in range(8)]
        RX = Region(RXt, RXN, "RX")
        RY = Region(RYt, RYN, "RY")

        P = Prog(nc)
        psk = [f"ps{i}" for i in range(8)]

        def psb(i):
            return ps[i][:, :].bitcast(BF16)

        def MM(out, lhsT, rhs, start, stop, reads, writes):
            P.op("pe", lambda e: e.matmul(out, lhsT=lhsT, rhs=rhs, start=start, stop=stop, skip_group_check=True),
                 reads, writes)

        def TR(out, in_, idn, reads, writes):
            P.op("pe", lambda e: e.transpose(out=out, in_=in_, identity=idn), reads, writes)

        def ACT(out, in_, func, reads, writes, scale=1.0, bias=None, accum=None):
            def f(e):
                kw = {}
                if bias is not None:
                    kw["bias"] = bias
                if accum is not None:
                    kw["accum_out"] = accum
                return e.activation(out=out, in_=in_, func=func, scale=scale, **kw)
            P.op("act", f, reads, writes)

        def TT(eng, out, in0, in1, op, reads, writes):
            P.op(eng, lambda e: e.tensor_tensor(out=out, in0=in0, in1=in1, op=op), reads, writes)

        def TS(eng, out, in0, s1, s2, op0, op1, reads, writes):
            if s2 is None:
                P.op(eng, lambda e: e.tensor_single_scalar(out=out, in_=in0, scalar=s1, op=op0), reads, writes)
            else:
                P.op(eng, lambda e: e.tensor_scalar(out=out, in0=in0, scalar1=s1, scalar2=s2, op0=op0, op1=op1),
                     reads, writes)

        def STT(eng, out, in0, scalar, in1, op0, op1, reads, writes):
            P.op(eng, lambda e: e.scalar_tensor_tensor(out=out, in0=in0, scalar=scalar, in1=in1, op0=op0, op1=op1),
                 reads, writes)

        def CP(eng, out, in_, reads, writes):
            if eng == "act":
                P.op("act", lambda e: e.activation(out=out, in_=in_, func=AF.Copy), reads, writes)
            else:
                P.op(eng, lambda e: e.tensor_copy(out=out, in_=in_), reads, writes)

        def MS(eng, ap, val, writes):
            P.op(eng, lambda e: e.memset(ap, val), (), writes)

        def ASEL(out, in_, pattern, cmp, fill, base, cm, reads, writes):
            P.op("pool", lambda e: e.affine_select(out=out, in_=in_, pattern=pattern, compare_op=cmp, fill=fill,
                                                   base=base, channel_multiplier=cm), reads, writes)

        dma_id = [0]

        def LD(out, in_, writes, sem=None, reads=(), q="sp", final=False):
            if sem is None:
                dma_id[0] += 1
                sem = f"m{dma_id[0]}"
            P.dma(q, sem, lambda e: e.dma_start(out=out, in_=in_), reads, writes, final=final)

        slab_ctr = [0]

        def take_slabs(n):
            ids = [(slab_ctr[0] + i) % 4 for i in range(n)]
            slab_ctr[0] += n
            return ids

        def load_slab(si, pieces):
            key = f"slab{si}"
            fns = []
            for (off, src) in pieces:
                nk = src.shape[0] // 128
                ncol = src.shape[1]
                fns.append(lambda e, off=off, src=src, nk=nk, ncol=ncol: e.dma_start(
                    out=slabs[si][:, 0:nk, off:off + ncol], in_=src.rearrange("(k p) c -> p k c", p=128)))
            if fns:
                P.dma_group("pool", key, fns, (), [key])

        units = []

        def unit(pieces_per_slab, fn):
            units.append((pieces_per_slab, fn))

        def run_units():
            pend = {}

            def issue(i):
                pieces_per_slab, fn = units[i]
                ids = take_slabs(len(pieces_per_slab))
                for si, pcs in zip(ids, pieces_per_slab):
                    load_slab(si, pcs)
                pend[i] = ids
            for j in range(min(2, len(units))):
                issue(j)
            for i in range(len(units)):
                if i + 2 < len(units):
                    issue(i + 2)
                units[i][1](pend[i])

        MS("pool", nhalf[:], -0.5, ["nhalf"])
        MS("pool", ident[:], 1.0, ["ident"])
        ASEL(ident[:], ident[:], [[-1, 128]], ALU.is_equal, 0.0, 0, 1, ["ident"], ["ident"])
        RX.reset()
        bc_b = RX.alloc([3 * D], F32)
        bc_gpre = RX.alloc([D], F32)
        bc_gpost = RX.alloc([D], F32)
        MA = RX.alloc([2, D], F32)
        MAs = RX.alloc([2, D], F32)
        xt = [RX.alloc([D], F32) for _ in range(3)]
        t1 = [RX.alloc([D], F32) for _ in range(2)]
        hb = [RX.alloc([D], BF16) for _ in range(2)]
        junk = [RX.alloc([D], BF16) for _ in range(2)]
        c_sb = RX.alloc([D], F32)
        c_th = RX.alloc([D], F32)
        c_bf = RX.alloc([D], BF16)
        cTp = RX.alloc([8, 128], BF16)
        cTs = RX.alloc([8, 32], BF16)
        kstgA = [RY.alloc([512], F32) for _ in range(6)]

        LD(c_sb[0:5, :], cc[:, :], ["c_sb"])
        LD(bc_b[:, :], b_cond.partition_broadcast(128), ["bc_b"])
        LD(bc_gpre[:, :], g_pre.partition_broadcast(128), ["bc_gpre"])
        LD(bc_gpost[:, :], g_post.partition_broadcast(128), ["bc_gpost"])
        ACT(c_th[0:5, :], c_sb[0:5, :], AF.Tanh, ["c_sb"], ["c_th"], scale=0.5)
        STT("dve", c_th[0:5, :], c_th[0:5, :], 1.0, c_sb[0:5, :], ALU.add, ALU.mult, ["c_th", "c_sb"], ["c_th"])
        TS("dve", c_bf[0:5, :], c_th[0:5, :], 0.5, None, ALU.mult, None, ["c_th"], ["c_bf"])
        pT7 = psb(7).rearrange("p (k c) -> p k c", c=128)
        for k in range(8):
            TR(pT7[:, k, 0:5], c_bf[0:5, k * 128:(k + 1) * 128], ident[0:5, 0:5], ["c_bf", "ident"], [psk[7]])
        CP("dve", cTp[:, :, :], pT7[:, :, 0:1].broadcast_to([128, 8, 128]), [psk[7]], ["cTp"])
        for b in range(4):
            CP("dve", cTs[:, :, 8 * b:8 * b + 8], pT7[:, :, 1 + b:2 + b].broadcast_to([128, 8, 8]), [psk[7]], ["cTs"])

        def mod_unit(j):
            def fn(ids):
                sl = slabs[ids[0]]
                sk = f"slab{ids[0]}"
                for k in range(8):
                    MM(ps[0][:, :], cTp[:, k, :], sl[:, k, :], k == 0, k == 7, ["cTp", sk], [psk[0]])
                for k in range(8):
                    MM(ps[1][0:32, :], cTs[:, k, :], sl[:, k, :], k == 0, k == 7, ["cTs", sk], [psk[1]])
                cols = slice(512 * j, 512 * j + 512)
                if j < 4:
                    dstp = MA.rearrange("p a b -> p (a b)")[:, cols]
                    dsts = MAs.rearrange("p a b -> p (a b)")[0:32, cols]
                    kp, ks_ = "MA", "MAs"
                else:
                    c2 = slice(512 * (j - 4), 512 * (j - 4) + 512)
                    dstp = GGp[:, c2]
                    dsts = GGs[0:32, c2]
                    kp, ks_ = "GGp", "GGs"
                TT("dve", dstp, ps[0][:, :], bc_b[:, cols], ALU.add, [psk[0], "bc_b"], [kp])
                TT("dve", dsts, ps[1][0:32, :], bc_b[0:32, cols], ALU.add, [psk[1], "bc_b"], [ks_])
            return fn

        for j in range(6):
            unit([[(0, w_cond[:, 512 * j:512 * j + 512])]], mod_unit(j))

        def mod_finish(ids):
            STT("dve", MA[:, 1, :], MA[:, 1, :], 1.0, bc_gpre[:, :], ALU.add, ALU.mult, ["MA", "bc_gpre"], ["MA"])
            STT("dve", MAs[0:32, 1, :], MAs[0:32, 1, :], 1.0, bc_gpre[0:32, :], ALU.add, ALU.mult, ["MAs", "bc_gpre"], ["MAs"])
            TT("pool", GGp[:, :], GGp[:, :], bc_gpost[:, :], ALU.mult, ["GGp", "bc_gpost"], ["GGp"])
            TT("pool", GGs[0:32, :], GGs[0:32, :], bc_gpost[0:32, :], ALU.mult, ["GGs", "bc_gpost"], ["GGs"])
            def s1_A1(i):
                n = 128 if i < 16 else 32
                xb_ = xt[i % 3]
                xk = f"xt{i % 3}"
                src = xp[i * 128:(i + 1) * 128, :] if i < 16 else xs[:, :]
                LD(xb_[0:n, :], src, [xk], sem=f"x{i % 3}")
                jk = f"junk{i % 2}"
                sk1 = f"s1_{i}"
                ACT(junk[i % 2][0:n, :], xb_[0:n, :], AF.Square, [xk], [jk, sk1], accum=ss[0:n, i:i + 1])
                TS("pool", st2[0:n, i:i + 1], ss[0:n, i:i + 1], 1.0 / D, EPS, ALU.mult, ALU.add, [sk1], [sk1])
                TT("pool", st2[0:n, i:i + 1], st2[0:n, i:i + 1], nhalf[0:n, 0:1], ALU.pow, [sk1, "nhalf"], [sk1])

            def s1_A2(i):
                n = 128 if i < 16 else 32
                xb_ = xt[i % 3]
                xk = f"xt{i % 3}"
                sk1 = f"s1_{i}"
                A_ = MA if i < 16 else MAs
                ak = "MA" if i < 16 else "MAs"
                tk1 = f"t1{i % 2}"
                STT("dve", t1[i % 2][0:n, :], xb_[0:n, :], st2[0:n, i:i + 1], A_[0:n, 1, :], ALU.mult, ALU.mult,
                    [xk, sk1, ak], [tk1])
                TT("dve", hb[i % 2][0:n, :], t1[i % 2][0:n, :], A_[0:n, 0, :], ALU.add, [tk1, ak], [f"hb{i % 2}"])

            def s1_B(i):
                n = 128 if i < 16 else 32
                hbi = hb[i % 2]
                hk = f"hb{i % 2}"
                pb = 6 + (i % 2)
                pTv = psb(pb).rearrange("p (k c) -> p k c", c=128)
                for k in range(8):
                    TR(pTv[:, k, 0:n], hbi[0:n, k * 128:(k + 1) * 128], ident[0:n, 0:n], [hk, "ident"], [psk[pb]])
                CP("act", hT[:, :, i * 128:i * 128 + n], pTv[:, :, 0:n], [psk[pb]], [f"hT{i}"])

            def s1_K(i):
                sl = slabs[ids[0]]
                sk = f"slab{ids[0]}"
                n = 128 if i < 16 else 32
                pb = i % 2
                for k in range(8):
                    MM(ps[pb][0:n, :], hT[:, k, i * 128:i * 128 + n], sl[:, k, :], k == 0, k == 7, [f"hT{i}", sk], [psk[pb]])
                stg = kstgA[i % 6]
                skey = f"kstgA{i % 6}"
                CP("act", stg[0:n, :], ps[pb][0:n, :], [psk[pb]], [skey])
                if i < 16:
                    LD(kvp[2][i * 128:i * 128 + 128, 0, :], stg[:, :], [], sem=skey, reads=[skey], final=True, q="act")
                else:
                    LD(kvs[2][:, 0, :], stg[0:32, :], [], sem=skey, reads=[skey], final=True, q="act")

            mgen = make_masks()
            for i in range(23):
                for _ in range(2):
                    next(mgen, None)
                if i < 17:
                    s1_A1(i)
                if 6 <= i < 23:
                    s1_K(i - 6)
                if 1 <= i < 18:
                    s1_A2(i - 1)
                if 2 <= i < 19:
                    s1_B(i - 2)
            for _ in mgen:
                pass

        unit([[(0, w_in[:, 4608 + 1024:4608 + 1536])]], mod_finish)

        def kout_unit(g):
            win, d = PAIRS[g]
            tiles = list(range(16 - win // 128, 16)) + [16]

            def fn(ids):
                sl = slabs[ids[0]]
                sk = f"slab{ids[0]}"
                for ti, i in enumerate(tiles):
                    n = 128 if i < 16 else 32
                    pb = ti % 2
                    for k in range(8):
                        MM(ps[pb][0:n, :], hT[:, k, i * 128:i * 128 + n], sl[:, k, :], k == 0, k == 7, ["hT", sk], [psk[pb]])
                    stg = kstg[ti % 4]
                    skey = f"kstg{ti % 4}"
                    CP("act" if ti % 2 == 0 else "dve", stg[0:n, :], ps[pb][0:n, :], [psk[pb]], [skey])
                    if i < 16:
                        r0 = i * 128 - (S - win)
                        LD(kvp[g][r0:r0 + 128, 0, :], stg[:, :], [], sem=skey, reads=[skey], final=True)
                    else:
                        LD(kvs[g][:, 0, :], stg[0:32, :], [], sem=skey, reads=[skey], final=True)
            return fn

        def barrier(regions):
            for rname in regions:
                c_ = 127 if rname == "RX" else 126
                MS("pool", ss[:, c_:c_ + 1], 0.0, [rname + "_ep"])

        RX.reset()
        kstg = [RX.alloc([512], F32) for _ in range(4)]
        for nm in ("kstg0", "kstg1"):
            P.alias[nm] = "RX_ep"
        qT = [RX.alloc([T], BF16) for _ in range(2)]
        kT = [RX.alloc([T], BF16) for _ in range(2)]
        Vaug = [RX.alloc([17, 2, 128], BF16) for _ in range(2)]
        PT = [RX.alloc([256], BF16) for _ in range(3)]
        Kc = [RX.alloc([8, 128], BF16) for _ in range(4)]
        Vc = [RX.alloc([8, 2, 128], BF16) for _ in range(4)]
        KTs = [RX.alloc([8, 128], BF16) for _ in range(4)]
        PTn = RX.alloc([64], BF16)
        PTc = [RX.alloc([16], BF16) for _ in range(4)]
        vstg = [RX.alloc([128], F32) for _ in range(4)]
        ztmp = RX.alloc([512], F32)
        RY.reset()
        ACC = [RY.alloc([T], F32) for _ in range(2)]
        SZB = [RY.alloc([T], BF16) for _ in range(2)]
        RD = RY.alloc([T], F32)
        s3keys = (["qT0", "qT1", "kT0", "kT1", "Vaug0", "Vaug1", "PT0", "PT1", "PT2", "Kc0", "Kc1", "Vc0", "Vc1",
                   "KTs", "PTn", "PTc", "vstg0", "vstg1", "vstg2", "vstg3", "ztmp"])
        for nm in s3keys:
            P.alias[nm] = "RX_ep"
        for nm in ("ACC0", "ACC1", "SZB", "RD"):
            P.alias[nm] = "RY_ep"
        for nm in ("bc_b", "bc_gpre", "bc_gpost", "MA", "MAs", "xt0", "xt1", "xt2", "t10", "t11", "hb0", "hb1", "junk0", "junk1", "c_sb", "c_th",
                   "c_bf", "cTp", "cTs"):
            P.alias[nm] = "RX_ep"

        def make_masks():
            MS("pool", MKN[:], 0.0, ["MKN"])
            yield
            for idx in (0, 2):
                ASEL(MKN[:, idx, :], MKN[:, idx, :], [[-1, 128]], ALU.is_ge, -30000.0, 0, 1, ["MKN"], ["MKN"])
                yield
            ASEL(MKN[:, 1, :], MKN[:, 1, :], [[1, 128]], ALU.is_ge, -30000.0, 0, -1, ["MKN"], ["MKN"])
            yield
            MS("pool", MC[:], 1.0, ["MC"])
            yield
            ASEL(MC[:], MC[:], [[-1, 8]], ALU.is_ge, 0.0, 0, 1, ["MC"], ["MC"])
            yield
            MS("pool", MN[:], 1.0, ["MN"])
            yield
            ASEL(MN[:, 0, :], MN[:, 0, :], [[1, 32]], ALU.is_ge, 0.0, 0, -1, ["MN"], ["MN"])
            yield
            ASEL(MN[:, 2, :], MN[:, 2, :], [[1, 32]], ALU.is_equal, 0.0, 0, -1, ["MN"], ["MN"])
            yield
            ASEL(MN[:, 1, :], MN[:, 1, :], [[1, 32]], ALU.is_equal, 0.0, -4, -1, ["MN"], ["MN"])
            yield
            for g in range(2):
                for b in range(4):
                    blk = MN[:, g, 8 * b:8 * b + 8]
                    ASEL(blk, blk, [[0, 8]], ALU.is_ge, 0.0, -8 * b, 1, ["MN"], ["MN"])
                    yield
                    ASEL(blk, blk, [[0, 8]], ALU.is_ge, 0.0, 8 * b + 7, -1, ["MN"], ["MN"])
                    yield
            TT("pool", MN[:, 1, :], MN[:, 1, :], MN[:, 2, :], ALU.add, ["MN"], ["MN"])
            yield
            MS("pool", tril[:], 1.0, ["tril"])
            yield
            ASEL(tril[:], tril[:], [[-1, 128]], ALU.is_ge, 0.0, 0, 1, ["tril"], ["tril"])
            yield


        def s3_begin(ids):
            barrier(["RX", "RY"])
            for b in range(2):
                MS("dve", Vaug[b][:, :, :, :], 1.0, [f"Vaug{b}"])
            for b in range(4):
                MS("dve", Vc[b][:, :, :, :], 1.0, [f"Vc{b}"])

        unit([], s3_begin)
        for g in range(2):
            unit([[(0, w_in[:, 4608 + 512 * g:4608 + 512 * g + 512])]], kout_unit(g))

        st_ring = [2, 3, 7]
        pj_ctr = [0]

        def next_pj():
            pj_ctr[0] += 1
            return pj_ctr[0] % 2

        def w_pieces(u):
            p, g = divmod(u, 3)
            pcs = [(0, w_in[:, 3072 + 512 * g + 128 * p:3072 + 512 * g + 128 * p + 128]),
                   (128, w_in[:, 4608 + 512 * g + 128 * p:4608 + 512 * g + 128 * p + 128]),
                   (256, w_in[:, 6144 + 512 * g + 128 * p:6144 + 512 * g + 128 * p + 128])]
            if g == 0:
                pcs.append((384, w_in[:, 7680 + 128 * p:7680 + 128 * p + 128]))
            return pcs

        def mk_tok(d):
            def tok(r, n, cnt):
                s0 = d * 128 * n + r
                return slice(s0, s0 + d * (cnt - 1) + 1, d) if d > 1 else slice(s0, s0 + cnt)
            return tok

        def proj_gen(u, sl, sk):
            p, g = divmod(u, 3)
            win, d = PAIRS[g]
            nb = 16 // d
            tok = mk_tok(d)
            ub = u % 2
            qk_, kk_, vk_ = f"qT{ub}", f"kT{ub}", f"Vaug{ub}"
            q_, k_, V_ = qT[ub], kT[ub], Vaug[ub]
            szb, szk = SZB[p % 2], f"SZB{p % 2}"

            def fm(coff, consume):
                for (c0, cn) in CH:
                    pb = next_pj()
                    for k in range(8):
                        MM(ps[pb][:, 0:cn], sl[:, k, coff:coff + 128], hT[:, k, c0:c0 + cn], k == 0, k == 7,
                           [sk, "hT"], [psk[pb]])
                    consume(pb, c0, cn)
                    yield

            if g == 0:
                def cons_z(pb, c0, cn):
                    ACT(ztmp[:, 0:cn], ps[pb][:, 0:cn], AF.Tanh, [psk[pb]], ["ztmp"], scale=0.5)
                    STT("dve", szb[:, c0:c0 + cn], ztmp[:, 0:cn], 1.0, ps[pb][:, 0:cn], ALU.add, ALU.mult,
                        ["ztmp", psk[pb]], [szk])
                yield from fm(384, cons_z)

            def perm_store(dst, key, pb, c0, cn):
                if d > 1 and cn == 512:
                    o = dst[:, 0:S].rearrange("p (r m) -> p r m", r=d)[:, :, c0 // d:(c0 + cn) // d]
                    i_ = ps[pb][:, 0:cn].rearrange("p (m r) -> p r m", r=d)
                    CP("act", o, i_, [psk[pb]], [key])
                else:
                    CP("act", dst[:, c0:c0 + cn], ps[pb][:, 0:cn], [psk[pb]], [key])

            def cons_q(pb, c0, cn):
                perm_store(q_, qk_, pb, c0, cn)

            def cons_k(pb, c0, cn):
                perm_store(k_, kk_, pb, c0, cn)
            yield from fm(0, cons_q)
            yield from fm(128, cons_k)
            blocks = [(r, n) for r in range(d) for n in range(nb)]
            for b0 in range(0, 16, 4):
                pb = next_pj()
                pv = ps[pb][:, :].rearrange("p (s c) -> p s c", c=128)
                for s_ in range(4):
                    r, n = blocks[b0 + s_]
                    for k in range(8):
                        MM(pv[:, s_, :], hT[:, k, tok(r, n, 128)], sl[:, k, 256:384], k == 0, k == 7,
                           ["hT", sk], [psk[pb]])
                CP("dve", V_[:, b0:b0 + 4, 0, 0:64], pv[:, :, 0:64], [psk[pb]], [vk_])
                CP("dve", V_[:, b0:b0 + 4, 1, 64:128], pv[:, :, 64:128], [psk[pb]], [vk_])
                for s_ in range(4):
                    r, n = blocks[b0 + s_]
                    if n == nb - 1:
                        vi = (b0 + s_) % 4
                        CP("dve", vstg[vi][:, :], pv[:, s_, :], [psk[pb]], [f"vstg{vi}"])
                        r0 = d * 128 * n + r - (S - win)
                        dst = kvp[g][r0:r0 + d * 127 + 1:d, 1, p * 128:(p + 1) * 128] if d > 1 else \
                            kvp[g][r0:r0 + 128, 1, p * 128:(p + 1) * 128]
                        LD(dst, vstg[vi][:, :], [], sem=f"vstg{vi}", reads=[f"vstg{vi}"], final=True)
                yield
            pb = next_pj()
            for k in range(8):
                MM(ps[pb][0:32, 0:128], hT[:, k, S:T], sl[:, k, 256:384], k == 0, k == 7, ["hT", sk], [psk[pb]])
            CP("dve", V_[0:32, 16, 0, 0:64], ps[pb][0:32, 0:64], [psk[pb]], [vk_])
            CP("dve", V_[0:32, 16, 1, 64:128], ps[pb][0:32, 64:128], [psk[pb]], [vk_])
            CP("dve", vstg[0][0:32, :], ps[pb][0:32, 0:128], [psk[pb]], ["vstg0"])
            LD(kvs[g][:, 1, p * 128:(p + 1) * 128], vstg[0][0:32, :], [], sem="vstg0", reads=["vstg0"], final=True)
            yield

        def attend_gen(u):
            p, g = divmod(u, 3)
            win, d = PAIRS[g]
            nb = 16 // d
            tok = mk_tok(d)
            ub = u % 2
            qk_, kk_, vk_ = f"qT{ub}", f"kT{ub}", f"Vaug{ub}"
            q_, k_, V_ = qT[ub], kT[ub], Vaug[ub]
            szb, szk = SZB[p % 2], f"SZB{p % 2}"
            nres = (1, 4, 8)[g]
            nq = (8, 2, 1)[g]
            osb = 6
            for b in range(4):
                kck, vck = f"Kc{b}", f"Vc{b}"
                csrc = ck[g][b].rearrange("(m r) t f -> m r t f", r=d)
                P.dma("pool", kck, lambda e, b=b, csrc=csrc: e.dma_start(
                    out=Kc[b][:, 0:nres, :], in_=csrc[:, 0:nres, 0, p * 128:(p + 1) * 128]), (), [kck])
                P.dma_group("pool", vck, [lambda e, b=b, csrc=csrc, e2=e2: e.dma_start(
                    out=Vc[b][:, 0:nres, e2, 64 * e2:64 * e2 + 64],
                    in_=csrc[:, 0:nres, 1, p * 128 + 64 * e2:p * 128 + 64 * e2 + 64]) for e2 in range(2)], (), [vck])
            if g < 2:
                rounds = [[(r, n0 + s_) for s_ in range(4)] for r in range(d) for n0 in range(0, nb, 4)]
            else:
                rounds = [[(r0 + s_, 0) for s_ in range(4)] for r0 in range(0, 16, 4)]
            items = []
            for rnd in rounds:
                for e in range(2):
                    sub = []
                    if g < 2:
                        r, n0 = rnd[0]
                        if n0 >= 1:
                            sub.append(((r, n0 - 1), 0, 1, MKN[:, 0, :]))
                        for s_ in range(4):
                            if s_ < 3:
                                sub.append(((r, n0 + s_), s_, 2, MKN[:, 1:3, :].rearrange("p a b -> p (a b)")))
                            else:
                                sub.append(((r, n0 + s_), s_, 1, MKN[:, 1, :]))
                    else:
                        for s_ in range(4):
                            sub.append((rnd[s_], s_, 1, MKN[:, 1, :]))
                    for ii, (kb, s0, nsl, msk) in enumerate(sub):
                        items.append(dict(rnd=rnd, e=e, kb=kb, s0=s0, nsl=nsl, msk=msk, first=(ii == 0),
                                          last=(ii == len(sub) - 1)))

            def emit_st(it, i):
                e = it["e"]
                hs = slice(64 * e, 64 * e + 64)
                N = 128 * it["nsl"]
                sbk = st_ring[i % 3]
                pti = i % 3
                kr, kn = it["kb"]
                qr, qn = it["rnd"][it["s0"]]
                LL = S // d
                MM(ps[sbk][:, 0:N], k_[hs, kr * LL + 128 * kn:kr * LL + 128 * kn + 128],
                   q_[hs, qr * LL + 128 * qn:qr * LL + 128 * qn + N], True, False, [kk_, qk_], [psk[sbk]])
                MM(ps[sbk][:, 0:N], ident[:, :], it["msk"], False, True, ["ident", "MKN"], [psk[sbk]])
                ACT(PT[pti][:, 0:N], ps[sbk][:, 0:N], AF.Exp, [psk[sbk]], [f"PT{pti}"], scale=0.125)

            def emit_pv(it, i):
                e = it["e"]
                ob = 4 + e
                N = 128 * it["nsl"]
                pti = i % 3
                kr, kn = it["kb"]
                s0 = it["s0"]
                rnd = it["rnd"]
                MM(ps[ob][:, 128 * s0:128 * s0 + N], V_[:, kr * nb + kn, e, :], PT[pti][:, 0:N], it["first"], False,
                   [vk_, f"PT{pti}"], [psk[ob]])
                if it["last"]:
                    if g == 0:
                        dst = ACC[e][:, 128 * rnd[0][1]:128 * rnd[0][1] + 512]
                        src = ps[ob][:, :]
                    elif g == 1:
                        r = rnd[0][0]
                        dst = ACC[e][:, r:S:4]
                        src = ps[ob][:, :]
                    else:
                        r0 = rnd[0][0]
                        dst = ACC[e][:, 0:S].rearrange("p (j r) -> p r j", r=16)[:, r0:r0 + 4, :]
                        src = ps[ob][:, :].rearrange("p (s c) -> p s c", c=128)
                    if g == 0:
                        CP("dve", dst, src, [psk[ob]], [f"ACC{e}"])
                    else:
                        TT("dve", dst, src, dst, ALU.add, [psk[ob], f"ACC{e}"], [f"ACC{e}"])

            SK = 2
            for i in range(len(items) + SK):
                if i < len(items):
                    emit_st(items[i], i)
                if i - SK >= 0:
                    emit_pv(items[i - SK], i - SK)
                yield

            def finalize(cs_):
                ACT(RD[0:64, cs_], ACC[0][64:128, cs_], AF.Ln, ["ACC0"], ["RD"])
                ACT(RD[64:128, cs_], ACC[1][0:64, cs_], AF.Ln, ["ACC1"], ["RD"])
                ACT(RD[:, cs_], RD[:, cs_], AF.Exp, ["RD"], ["RD"], scale=-1.0)
                STT("dve", RD[:, cs_], RD[:, cs_], 0.5, szb[:, cs_], ALU.mult, ALU.mult, ["RD", szk], ["RD"])
                TT("dve", ybT[0:64, p, cs_], ACC[0][0:64, cs_], RD[0:64, cs_], ALU.mult, ["ACC0", "RD"], ["ybT"])
                TT("pool", ybT[64:128, p, cs_], ACC[1][64:128, cs_], RD[64:128, cs_], ALU.mult, ["ACC1", "RD"], ["ybT"])

            sbk = st_ring[0]
            for e in range(2):
                hs = slice(64 * e, 64 * e + 64)
                MM(ps[sbk][0:32, 32 * e:32 * e + 32], k_[hs, S:T], q_[hs, S:T], True, True, [kk_, qk_],
                   [psk[sbk], "pe_ser"])
            ACT(PTn[0:32, :], ps[sbk][0:32, 0:64], AF.Exp, [psk[sbk]], ["PTn"], scale=0.125)
            mn = MN[:, g, :]
            mn2 = bass.AP(tensor=mn.tensor, offset=mn.offset, ap=[list(mn.ap[0]), [0, 2], list(mn.ap[1])])
            TT("dve", PTn[0:32, :].rearrange("p (e q) -> p e q", e=2), PTn[0:32, :].rearrange("p (e q) -> p e q", e=2),
               mn2, ALU.mult, ["PTn", "MN"], ["PTn"])
            for e in range(2):
                MM(ps[osb][:, 32 * e:32 * e + 32], V_[0:32, 16, e, :], PTn[0:32, 32 * e:32 * e + 32],
                   (g == 0 and e == 0), False, [vk_, "PTn"], [psk[osb]])
            yield
            for b in range(4):
                pb = next_pj()
                pTv = psb(pb).rearrange("p (k c) -> p k c", c=128)
                for r in range(nres):
                    TR(pTv[:, r, :], Kc[b][:, r, :], ident[:, :], [f"Kc{b}", "ident"], [psk[pb]])
                CP("dve", KTs[b][:, 0:nres, :], pTv[:, 0:nres, :], [psk[pb]], [f"KTs{b}"])
                yield
            for b in range(4):
                sbk = st_ring[(b + 1) % 3]
                stv = ps[sbk][:, 0:16].rearrange("p (r e q) -> p r e q", e=2, q=nq)
                for e in range(2):
                    for r in range(nres):
                        hs = slice(64 * e, 64 * e + 64)
                        q0 = S + 8 * b + (r if g > 0 else 0)
                        qs = slice(q0, q0 + 5, 4) if g == 1 else slice(q0, q0 + nq)
                        ser = ["pe_ser"] if (r == nres - 1 and e == 0) or (r == 0 and e == 1) else []
                        MM(stv[:, r, e, :], KTs[b][hs, r, :], q_[hs, qs], True, True, [f"KTs{b}", qk_], [psk[sbk]] + ser)
                ACT(PTc[b][:, :], ps[sbk][:, 0:16], AF.Exp, [psk[sbk]], [f"PTc{b}"], scale=0.125)
                mc = bass.AP(tensor=MC, offset=0, ap=[[8, 128], [0, nres * 2], [1, nq]])
                ptv = PTc[b][:, :].rearrange("p (a q) -> p a q", q=nq)
                TT("dve", ptv, ptv, mc, ALU.mult, [f"PTc{b}", "MC"], [f"PTc{b}"])
                yield
            for b in range(4):
                ptv4 = PTc[b][:, :].rearrange("p (r e q) -> p r e q", e=2, q=nq)
                for r in range(nres):
                    for e in range(2):
                        o0 = 32 * e + 8 * b + (r if g > 0 else 0)
                        osl = slice(o0, o0 + 5, 4) if g == 1 else slice(o0, o0 + nq)
                        MM(ps[osb][:, osl], Vc[b][:, r, e, :], ptv4[:, r, e, :], False, False, [f"Vc{b}", f"PTc{b}"], [psk[osb]])
                yield
            if g == 2:
                for e in range(2):
                    CP("act", ACC[e][:, S:T], ps[osb][:, 32 * e:32 * e + 32], [psk[osb]], [f"ACC{e}"])
                finalize(slice(0, T))
                yield

        def run_interleaved(a, b, rates):
            done_b = b is None
            acc = 0.0
            for i, _ in enumerate(a):
                if not done_b:
                    acc += rates[i] if i < len(rates) else 1.0
                    while acc >= 1.0 and not done_b:
                        acc -= 1.0
                        try:
                            next(b)
                        except StopIteration:
                            done_b = True
            if not done_b:
                for _ in b:
                    pass

        NU = 12

        def attn_pre(ids):
            for _ in proj_gen(0, slabs[ids[0]], f"slab{ids[0]}"):
                pass

        def attn_unit(u):
            def fn(ids):
                nxt = proj_gen(u + 1, slabs[ids[0]], f"slab{ids[0]}") if u + 1 < NU else None
                g = u % 3
                npr = (40 if g == 0 else 32) + 2
                nsm = 13 + (1 if g == 2 else 0)
                nbg = 20 if (u + 1) % 3 == 0 else 15
                r2 = 0.4
                r1 = max(nbg - r2 * (nsm - 3), 0.0) / npr
                run_interleaved(attend_gen(u), nxt, [r1] * npr + [r2] * nsm)
            return fn

        unit([w_pieces(0)], attn_pre)
        for u in range(NU):
            unit([w_pieces(u + 1)] if u + 1 < NU else [], attn_unit(u))

        RX.reset()
        vn = RX.alloc([17, D], BF16)
        lng = RX.alloc([D], F32)
        lnb = RX.alloc([D], F32)
        va = [RX.alloc([D], F32) for _ in range(3)]
        va2 = RX.alloc([D], BF16)
        bsp = RX.alloc([8, 128], F32)
        wspT = RX.alloc([8, 128], BF16)
        wspTs = RX.alloc([8, 32], BF16)
        wspn = RX.alloc([8, 128], F32)
        wspb = RX.alloc([8, 128], BF16)
        wsps = RX.alloc([8, 32], F32)
        wspsb = RX.alloc([8, 32], BF16)
        tmpA = [[RX.alloc([512], F32) for _ in range(3)] for _ in range(2)]
        hbsp = bsp
        RY.reset()
        yaT = RY.alloc([8, T], BF16)
        for nm in (["vn", "lng", "lnb", "va0", "va1", "va2", "vb2", "bsp", "hbsp", "wspT", "wspTs", "wspn", "wspb", "wsps", "wspsb"]
                   + [f"tA{i}{j}" for i in range(2) for j in range(5)]):
            P.alias[nm] = "RX_ep"
        P.alias["yaT"] = "RY_ep"

        def s2_begin(ids):
            barrier(["RX", "RY"])
            LD(lng[:, :], ln_g.partition_broadcast(128), ["lng"])
            LD(lnb[:, :], ln_b.partition_broadcast(128), ["lnb"])
            LD(bsp[:, :, :].rearrange("p g t -> p (g t)"), b_sp.rearrange("g t -> (g t)").partition_broadcast(128), ["bsp"])
            LD(wspn[:, :, :], w_sp.rearrange("g t s -> t g s"), ["wspn"])
            MS("pool", wsps[0:32, :, :], 0.0, ["wsps"])
            for b in range(4):
                LD(wsps[8 * b:8 * b + 8, :, 8 * b:8 * b + 8], w_sp[:, 0:8, 0:8].rearrange("g t s -> t g s"), ["wsps"])

        def wsp_prep():
            TS("dve", bsp[:, :, :], bsp[:, :, :], 0.5, None, ALU.mult, None, ["bsp"], ["bsp"])
            tri3 = bass.AP(tensor=tril.tensor if hasattr(tril, "tensor") else tril, offset=0, ap=[[128, 128], [0, 8], [1, 128]])
            TT("dve", wspb[:, :, :], wspn[:, :, :], tri3, ALU.mult, ["wspn", "tril"], ["wspb"])
            pTv = psb(6).rearrange("p (k c) -> p k c", c=128)
            for g8 in range(8):
                TR(pTv[:, g8, :], wspb[:, g8, :], ident[:, :], ["wspb", "ident"], [psk[6]])
            CP("dve", wspT[:, :, :], pTv[:, :, :], [psk[6]], ["wspT"])
            tri3s = bass.AP(tensor=tril.tensor if hasattr(tril, "tensor") else tril, offset=0, ap=[[128, 32], [0, 8], [1, 32]])
            TT("dve", wspsb[0:32, :, :], wsps[0:32, :, :], tri3s, ALU.mult, ["wsps", "tril"], ["wspsb"])
            pTs = psb(7).rearrange("p (k c) -> p k c", c=128)
            for g8 in range(8):
                TR(pTs[0:32, g8, 0:32], wspsb[0:32, g8, :], ident[0:32, 0:32], ["wspsb", "ident"], [psk[7]])
            CP("dve", wspTs[0:32, :, :], pTs[0:32, :, 0:32], [psk[7]], ["wspTs"])

        unit([], s2_begin)

        def va_unit(ids):
            def ph1(i):
                n = 128 if i < 16 else 32
                pb0 = 2 * (i % 3)
                for h in range(2):
                    sl = slabs[ids[h]]
                    sk = f"slab{ids[h]}"
                    for k in range(8):
                        MM(ps[pb0 + h][0:n, :], hT[:, k, i * 128:i * 128 + n], sl[:, k, :], k == 0, k == 7,
                           ["hT", sk], [psk[pb0 + h]])
                v_ = va[i % 3]
                vk = f"va{i % 3}"
                c0 = 20 + 2 * i
                lk = f"ln_{i}"
                bst = bnst[i % 3]
                for h in range(2):
                    P.op("dve", lambda e, h=h, bst=bst, n=n, pb0=pb0: e.bn_stats(out=bst[0:n, 6 * h:6 * h + 6], in_=ps[pb0 + h][0:n, :]),
                         [psk[pb0 + h]], [lk])
                P.op("dve", lambda e, bst=bst, n=n, c0=c0: e.bn_aggr(out=ss[0:n, c0:c0 + 2], in_=bst[0:n, 0:12]), [lk], [lk])
                TS("dve", st2[0:n, c0:c0 + 1], ss[0:n, c0 + 1:c0 + 2], EPS, None, ALU.add, None, [lk], [lk])
                TT("pool", st2[0:n, c0:c0 + 1], st2[0:n, c0:c0 + 1], nhalf[0:n, 0:1], ALU.pow, [lk, "nhalf"], [lk])
                STT("dve", st2[0:n, c0 + 1:c0 + 2], ss[0:n, c0:c0 + 1], -1.0, st2[0:n, c0:c0 + 1], ALU.mult, ALU.mult, [lk], [lk])
                for h in range(2):
                    ACT(v_[0:n, 512 * h:512 * h + 512], ps[pb0 + h][0:n, :], AF.Identity, [psk[pb0 + h], lk], [vk],
                        scale=st2[0:n, c0:c0 + 1], bias=st2[0:n, c0 + 1:c0 + 2])

            def ph2(i):
                n = 128 if i < 16 else 32
                v_ = va[i % 3]
                vk = f"va{i % 3}"
                c0 = 20 + 2 * i
                lk = f"ln_{i}"
                TT("dve", v_[0:n, :], v_[0:n, :], lng[0:n, :], ALU.mult, [vk, "lng"], [vk])
                if i < 16:
                    TT("pool" if i % 2 == 0 else "dve", vn[0:n, i, :], v_[0:n, :], lnb[0:n, :], ALU.add, [vk, "lnb"], [f"vn{i}"])
                else:
                    TT("pool", v_[0:n, :], v_[0:n, :], lnb[0:n, :], ALU.add, [vk, "lnb"], [vk])
                    CP("dve", vn[0:n, i, :], v_[0:n, :], [vk], [f"vn{i}"])
                    LD(vch[:, :], v_[0:32, :], [], sem="vch", reads=[vk], final=True)

            for i in range(18):
                if i < 17:
                    ph1(i)
                if i >= 1:
                    ph2(i - 1)
                if i == 12:
                    wsp_prep()

        unit([[(0, w_in[:, 1024:1536])], [(0, w_in[:, 1536:2048])]], va_unit)

        def ya_unit(g8):
            def fn(ids):
                sl = slabs[ids[0]]
                sk = f"slab{ids[0]}"
                for ci, (c0, cn) in enumerate(CH):
                    b3 = 3 * (ci % 2)
                    U, Z, ZS = b3, b3 + 1, b3 + 2
                    for k in range(8):
                        MM(ps[U][:, 0:cn], sl[:, k, 0:128], hT[:, k, c0:c0 + cn], k == 0, k == 7, [sk, "hT"], [psk[U]])
                    for k in range(8):
                        MM(ps[Z][:, 0:cn], sl[:, k, 128:256], hT[:, k, c0:c0 + cn], k == 0, k == 7, [sk, "hT"], [psk[Z]])
                    if cn == 512:
                        for s_ in range(4):
                            i = c0 // 128 + s_
                            MM(ps[ZS][:, 128 * s_:128 * s_ + 128], vn[:, i, g8 * 128:(g8 + 1) * 128], wspT[:, g8, :],
                               True, True, ["vn", "wspT"], [psk[ZS]])
                        bias = bass.AP(tensor=hbsp.tensor, offset=hbsp[:, g8, :].offset, ap=[list(hbsp.ap[0]), [0, 4], [1, 128]])
                        zs_in = ps[ZS][:, :].rearrange("p (s c) -> p s c", c=128)
                    else:
                        MM(ps[ZS][:, 0:32], vn[0:32, 16, g8 * 128:(g8 + 1) * 128], wspTs[0:32, g8, :], True, True,
                           ["vn", "wspTs"], [psk[ZS]])
                        bias = bass.AP(tensor=hbsp.tensor, offset=hbsp[:, g8, :].offset, ap=[list(hbsp.ap[0]), [0, 4], [1, 8]])
                        zs_in = ps[ZS][:, 0:32].rearrange("p (s c) -> p s c", c=8)
                    tt_ = tmpA[ci % 2]
                    tk = [f"tA{ci % 2}{j}" for j in range(3)]
                    cs = cn // 4
                    ACT(tt_[0][:, 0:cn], ps[Z][:, 0:cn], AF.Tanh, [psk[Z]], [tk[0]], scale=0.5)
                    STT("dve", tt_[0][:, 0:cn], tt_[0][:, 0:cn], 1.0, ps[Z][:, 0:cn], ALU.add, ALU.mult, [tk[0], psk[Z]], [tk[0]])
                    STT("dve", tt_[1][:, 0:cn].rearrange("p (s c) -> p s c", c=cs), zs_in, 0.5, bias, ALU.mult, ALU.add,
                        [psk[ZS], "bsp"], [tk[1]])
                    TT("dve", tt_[1][:, 0:cn], tt_[1][:, 0:cn], ps[U][:, 0:cn], ALU.mult, [tk[1], psk[U]], [tk[1]])
                    TT("pool", yaT[:, g8, c0:c0 + cn], tt_[1][:, 0:cn], tt_[0][:, 0:cn], ALU.mult, [tk[1], tk[0]], ["yaT"])
            return fn

        for g8 in range(8):
            unit([[(0, w_in[:, 128 * g8:128 * g8 + 128]), (128, w_in[:, 2048 + 128 * g8:2048 + 128 * g8 + 128])]], ya_unit(g8))

        RX4 = Region(RXt, RXN, "RX4")
        mgT = RX4.alloc([8, T], BF16)
        tmpM = [[RX4.alloc([512], F32) for _ in range(6)] for _ in range(2)]
        xt4 = [RX4.alloc([D], F32) for _ in range(2)]
        ot4 = [RX4.alloc([D], F32) for _ in range(2)]
        sq4 = RX4.alloc([512], F32)
        for nm in (["mgT", "x40", "x41", "o40", "o41", "sq4"] + [f"tM{i}{j}" for i in range(2) for j in range(6)]):
            P.alias[nm] = "RX_ep"

        def s4_begin(ids):
            barrier(["RX"])

        unit([], s4_begin)

        def mg_unit(j):
            def fn(ids):
                sl = slabs[ids[0]]
                sk = f"slab{ids[0]}"
                for ci, (c0, cn) in enumerate(CH):
                    b4 = 4 * (ci % 2)
                    PA, PB, GA, GB = b4, b4 + 1, b4 + 2, b4 + 3
                    for k in range(8):
                        MM(ps[PA][:, 0:cn], sl[:, k, 0:128], yaT[:, k, c0:c0 + cn], k == 0, k == 7, [sk, "yaT"], [psk[PA]])
                    for k in range(4):
                        MM(ps[PB][:, 0:cn], sl[:, k, 128:256], ybT[:, k, c0:c0 + cn], k == 0, k == 3, [sk, "ybT"], [psk[PB]])
                    for k in range(8):
                        MM(ps[GA][:, 0:cn], sl[:, k, 256:384], hT[:, k, c0:c0 + cn], k == 0, k == 7, [sk, "hT"], [psk[GA]])
                    for k in range(8):
                        MM(ps[GB][:, 0:cn], sl[:, k, 384:512], hT[:, k, c0:c0 + cn], k == 0, k == 7, [sk, "hT"], [psk[GB]])
                    tt_ = tmpM[ci % 2]
                    tk = [f"tM{ci % 2}{q}" for q in range(6)]
                    ACT(tt_[0][:, 0:cn], ps[GA][:, 0:cn], AF.Tanh, [psk[GA]], [tk[0]], scale=0.5)
                    ACT(tt_[1][:, 0:cn], ps[GB][:, 0:cn], AF.Tanh, [psk[GB]], [tk[1]], scale=0.5)
                    TS("pool", tt_[2][:, 0:cn], tt_[0][:, 0:cn], 0.5, 0.5, ALU.mult, ALU.add, [tk[0]], [tk[2]])
                    TS("pool", tt_[3][:, 0:cn], tt_[1][:, 0:cn], 0.5, 0.5, ALU.mult, ALU.add, [tk[1]], [tk[3]])
                    TT("dve", tt_[4][:, 0:cn], ps[PA][:, 0:cn], tt_[2][:, 0:cn], ALU.mult, [psk[PA], tk[2]], [tk[4]])
                    TT("dve", tt_[5][:, 0:cn], ps[PB][:, 0:cn], tt_[3][:, 0:cn], ALU.mult, [psk[PB], tk[3]], [tk[5]])
                    TT("pool", mgT[:, j, c0:c0 + cn], tt_[4][:, 0:cn], tt_[5][:, 0:cn], ALU.add, [tk[4], tk[5]], ["mgT"])
            return fn

        for j in range(8):
            unit([[(0, w_pa[:, 128 * j:128 * j + 128]), (128, w_pb[:, 128 * j:128 * j + 128]),
                   (256, w_in[:, 8192 + 128 * j:8192 + 128 * j + 128]),
                   (384, w_in[:, 9216 + 128 * j:9216 + 128 * j + 128])]], mg_unit(j))

        def out_unit(ids):
            def ph1(i):
                n = 128 if i < 16 else 32
                pb0 = 2 * (i % 3)
                xk = f"x4{i % 2}"
                src = xp[i * 128:(i + 1) * 128, :] if i < 16 else xs[:, :]
                LD(xt4[i % 2][0:n, :], src, [xk], sem=xk)
                for h in range(2):
                    sl = slabs[ids[h]]
                    sk = f"slab{ids[h]}"
                    for k in range(8):
                        MM(ps[pb0 + h][0:n, :], mgT[:, k, i * 128:i * 128 + n], sl[:, k, :], k == 0, k == 7,
                           ["mgT", sk], [psk[pb0 + h]])
                c0 = 64 + 2 * i
                fk = f"fst_{i}"
                for h in range(2):
                    ACT(sq4[0:n, :], ps[pb0 + h][0:n, :], AF.Square, [psk[pb0 + h]], ["sq4", fk], accum=ss[0:n, c0 + h:c0 + h + 1])
                TT("dve", st2[0:n, c0:c0 + 1], ss[0:n, c0:c0 + 1], ss[0:n, c0 + 1:c0 + 2], ALU.add, [fk], [fk])
                TS("dve", st2[0:n, c0:c0 + 1], st2[0:n, c0:c0 + 1], 1.0 / D, EPS, ALU.mult, ALU.add, [fk], [fk])
                TT("pool", st2[0:n, c0:c0 + 1], st2[0:n, c0:c0 + 1], nhalf[0:n, 0:1], ALU.pow, [fk, "nhalf"], [fk])

            def ph2(i):
                n = 128 if i < 16 else 32
                pb0 = 2 * (i % 3)
                xk, ok = f"x4{i % 2}", f"o4{i % 2}"
                c0 = 64 + 2 * i
                fk = f"fst_{i}"
                G_ = GGp if i < 16 else GGs
                gk = "GGp" if i < 16 else "GGs"
                o_ = ot4[i % 2]
                for h in range(2):
                    cs = slice(512 * h, 512 * h + 512)
                    STT("dve", o_[0:n, cs], ps[pb0 + h][0:n, :], st2[0:n, c0:c0 + 1], G_[0:n, cs], ALU.mult, ALU.mult,
                        [psk[pb0 + h], fk, gk], [ok])
                TT("pool" if i % 2 == 0 else "dve", o_[0:n, :], o_[0:n, :], xt4[i % 2][0:n, :], ALU.add, [ok, xk], [ok])
                dst = yp[i * 128:(i + 1) * 128, :] if i < 16 else ys[:, :]
                LD(dst, o_[0:n, :], [], sem=ok, reads=[ok], final=True)

            for i in range(18):
                if i < 17:
                    ph1(i)
                if i >= 1:
                    ph2(i - 1)

        unit([[(0, w_out[:, 0:512])], [(0, w_out[:, 512:1024])]], out_unit)

        run_units()
        P.emit()
    return nc


_CACHE = {}


def kernel(x_prompt, x_sample, cache_kv_w128, cache_kv_w512, cache_kv_w2048, c_prompt, c_sample,
           w_cond, b_cond, g_pre, w_in, ln_v_g, ln_v_b, w_spatial, b_spatial,
           w_proj_a, w_proj_b, w_out, g_post):
    f = lambda a: np.ascontiguousarray(np.asarray(a, dtype=np.float32))
    if "nc" not in _CACHE:
        _CACHE["nc"] = build_nc()
    nc = _CACHE["nc"]
    x_prompt, x_sample = f(x_prompt), f(x_sample)
    caches = [f(cache_kv_w128), f(cache_kv_w512), f(cache_kv_w2048)]
    c_prompt, c_sample = f(c_prompt), f(c_sample)
    shared = {
        "w_cond": f(w_cond)[0], "b_cond": f(b_cond)[0], "g_pre": f(g_pre)[0], "w_in": f(w_in)[0],
        "ln_v_g": f(ln_v_g)[0], "ln_v_b": f(ln_v_b)[0], "w_spatial": f(w_spatial)[0], "b_spatial": f(b_spatial)[0],
        "w_proj_a": f(w_proj_a)[0], "w_proj_b": f(w_proj_b)[0], "w_out": f(w_out)[0], "g_post": f(g_post)[0],
    }
    in_maps = []
    for i in range(8):
        m = dict(shared)
        m["xp"] = x_prompt[i]
        m["xs"] = x_sample[4 * i:4 * i + 4].reshape(NS, D)
        m["cc"] = np.concatenate([c_prompt[i:i + 1], c_sample[4 * i:4 * i + 4]], axis=0)
        for g in range(3):
            c = caches[g][0, 4 * i:4 * i + 4]
            m[f"ck{g}"] = np.ascontiguousarray(c.reshape(4, c.shape[1], 2, 512))
        in_maps.append(m)
    res = run_bass_kernel_spmd(nc, in_maps, core_ids=list(range(8)))
    R = res.results
    y_p = np.stack([R[i]["yp"] for i in range(8)], axis=0)
    y_s = np.concatenate([R[i]["ys"].reshape(4, 8, D) for i in range(8)], axis=0)
    outs = [y_p, y_s]
    for g in range(3):
        win = PAIRS[g][0]
        outs.append(np.stack([R[i][f"kvp{g}"].reshape(win, 2, 8, 64) for i in range(8)], axis=0)[None])
    for g in range(3):
        outs.append(np.concatenate([R[i][f"kvs{g}"].reshape(4, 8, 2, 8, 64) for i in range(8)], axis=0)[None])
    outs.append(np.concatenate([R[i]["vch"].reshape(4, 8, D) for i in range(8)], axis=0)[None])
    return tuple(np.ascontiguousarray(o.astype(np.float32)) for o in outs)
```

```python
import contextlib
import numpy as np
import concourse.bass as bass
import concourse.mybir as mybir
from concourse.bass_utils import run_bass_kernel_spmd

F32 = mybir.dt.float32
BF16 = mybir.dt.bfloat16
AF = mybir.ActivationFunctionType
ALU = mybir.AluOpType

D = 1024
S = 2048
NS = 32
T = S + NS
EPS = 1e-6
PAIRS = ((128, 1), (512, 4), (2048, 16))
CH = [(0, 512), (512, 512), (1024, 512), (1536, 512), (2048, 32)]
COMPUTE = ("pe", "act", "dve", "pool")


RXKEYS = set([f"vn{i}" for i in range(17)] + ["bc_b", "bc_gpre", "bc_gpost", "MA", "MAs", "xt0", "xt1", "xt2", "t10", "t11", "hb0", "hb1", "junk0", "junk1", "c_sb", "c_th",
              "c_bf", "cTp", "cTs", "kstg0", "kstg1", "kstg2", "kstg3",
              "qT0", "qT1", "kT0", "kT1", "Vaug0", "Vaug1", "PT0", "PT1", "PT2", "Kc0", "Kc1", "Kc2", "Kc3", "Vc0", "Vc1", "Vc2", "Vc3",
              "KTs0", "KTs1", "KTs2", "KTs3", "PTn", "PTc0", "PTc1", "PTc2", "PTc3", "vstg0", "vstg1", "vstg2", "vstg3", "ztmp", "rscr",
              "vn", "lng", "lnb", "va0", "va1", "va2", "vb2", "bsp", "hbsp", "wspT", "wspTs", "wspn", "wspb", "wsps", "wspsb",
              "mgT", "x40", "x41", "o40", "o41", "sq4"]
             + [f"tA{i}{j}" for i in range(2) for j in range(5)] + [f"tM{i}{j}" for i in range(2) for j in range(6)])
RYKEYS = set(["ACC0", "ACC1", "SZB0", "SZB1", "RD", "yaT"] + [f"kstgA{i}" for i in range(6)])

class Prog:
    def __init__(self, nc):
        self.nc = nc
        self.ops = {e: [] for e in ("pe", "act", "dve", "pool", "sp")}
        self.cnt = {e: 0 for e in COMPUTE}
        self.seen = {e: {} for e in self.ops}
        self.last_w = {}
        self.readers = {}
        self.dma_cnt = {}
        self.final = {}
        self.alias = {}

    def _expand(self, reads, writes):
        reads = list(reads)
        writes = list(writes)
        for big in ("hT", "vn"):
            if big in reads:
                reads = [k for k in reads if k != big] + [f"{big}{i}" for i in range(17)]
        extra = []
        for k in reads + writes:
            a = "RX_ep" if k in RXKEYS else ("RY_ep" if k in RYKEYS else None)
            if a is not None and a not in extra:
                extra.append(a)
        writes = writes + [k for k in reads if k.startswith("ps") and k not in writes]
        return reads + extra, writes

    def _deps(self, eng, reads, writes):
        need = {}

        def add(tok, kind, key=None):
            sk, val, peng = tok
            if peng == eng and eng == "pe" and key != "pe_ser":
                return
            if need.get(sk, 0) < val:
                need[sk] = val

        for k in reads:
            w = self.last_w.get(k)
            if w is not None:
                add(w, "raw", k)
        for k in writes:
            w = self.last_w.get(k)
            if w is not None:
                add(w, "waw", k)
            for (sk, peng), val in self.readers.get(k, {}).items():
                add((sk, val, peng), "war")
        out = []
        for sk, val in need.items():
            if self.seen[eng].get(sk, 0) >= val:
                continue
            self.seen[eng][sk] = val
            out.append((sk, val))
        return out

    def _commit(self, tok, reads, writes):
        for k in writes:
            self.last_w[k] = tok
            self.readers[k] = {}
        sk, val, peng = tok
        for k in reads:
            d = self.readers.setdefault(k, {})
            if d.get((sk, peng), 0) < val:
                d[(sk, peng)] = val

    def op(self, eng, fn, reads=(), writes=()):
        reads, writes = self._expand(reads, writes)
        waits = self._deps(eng, reads, writes)
        self.cnt[eng] += 1
        tok = (eng, self.cnt[eng], eng)
        self.ops[eng].append((fn, waits, ("eng", eng)))
        self._commit(tok, reads, writes)
        return tok

    def dma(self, eng, sem, fn, reads=(), writes=(), final=False):
        reads, writes = self._expand(reads, writes)
        waits = self._deps(eng, reads, writes)
        self.dma_cnt[sem] = self.dma_cnt.get(sem, 0) + 16
        tok = ("dma:" + sem, self.dma_cnt[sem], "dma")
        self.ops[eng].append((fn, waits, ("dma", sem)))
        self._commit(tok, reads, writes)
        if final:
            self.final["dma:" + sem] = self.dma_cnt[sem]
        return tok

    def dma_group(self, eng, sem, fns, reads=(), writes=(), final=False):
        reads, writes = self._expand(reads, writes)
        waits = self._deps(eng, reads, writes)
        for j, fn in enumerate(fns):
            self.dma_cnt[sem] = self.dma_cnt.get(sem, 0) + 16
            self.ops[eng].append((fn, waits if j == 0 else [], ("dma", sem)))
        tok = ("dma:" + sem, self.dma_cnt[sem], "dma")
        self._commit(tok, reads, writes)
        if final:
            self.final["dma:" + sem] = self.dma_cnt[sem]
        return tok

    def emit(self):
        nc = self.nc
        targets = {e: set() for e in COMPUTE}
        for engname, lst in self.ops.items():
            for fn, waits, kind in lst:
                for sk, val in waits:
                    if sk in targets:
                        targets[sk].add(val)
        for e in COMPUTE:
            if self.cnt[e]:
                targets[e].add(self.cnt[e])
        rank = {}
        for e in COMPUTE:
            for r, idx in enumerate(sorted(targets[e])):
                rank[(e, idx)] = r + 1
        with contextlib.ExitStack() as st:
            sems = {}
            for e in COMPUTE:
                sems[e] = st.enter_context(nc.semaphore("s_" + e))
            for name in self.dma_cnt:
                sems["dma:" + name] = st.enter_context(nc.semaphore("d_" + name))
            block = st.enter_context(nc.Block())
            prog = self

            def run(engname, eng):
                idx = 0
                for fn, waits, kind in prog.ops[engname]:
                    for sk, val in waits:
                        if sk in targets:
                            eng.wait_ge(sems[sk], rank[(sk, val)])
                        else:
                            eng.wait_ge(sems[sk], val)
                    ins = fn(eng)
                    if kind[0] == "eng":
                        idx += 1
                        if (kind[1], idx) in rank:
                            ins.then_inc(sems[kind[1]], 1)
                    else:
                        ins.then_inc(sems["dma:" + kind[1]], 16)

            @block.tensor
            def _(eng):
                run("pe", eng)

            @block.scalar
            def _(eng):
                run("act", eng)

            @block.vector
            def _(eng):
                run("dve", eng)

            @block.gpsimd
            def _(eng):
                run("pool", eng)

            @block.sync
            def _(eng):
                run("sp", eng)
                for sk, val in prog.final.items():
                    eng.wait_ge(sems[sk], val)
                for e in COMPUTE:
                    if prog.cnt[e]:
                        eng.wait_ge(sems[e], rank[(e, prog.cnt[e])])


class Region:
    def __init__(self, tensor, nelem, name):
        self.t = tensor
        self.n = nelem
        self.name = name
        self.off = 0

    def reset(self):
        self.off = 0

    def alloc(self, free_shape, dt):
        n = int(np.prod(free_shape))
        nb = n * (2 if dt == F32 else 1)
        nb = (nb + 1) // 2 * 2
        assert self.off + nb <= self.n, (self.name, self.off, nb, self.n)
        v = self.t[:, self.off:self.off + nb]
        self.off += nb
        if dt == F32:
            v = v.bitcast(F32)
        if len(free_shape) == 2:
            v = v.rearrange("p (a b) -> p a b", b=free_shape[1])
        elif len(free_shape) == 3:
            v = v.rearrange("p (a b c) -> p a b c", b=free_shape[1], c=free_shape[2])
        return v


def build_nc(debug=False):
    nc = bass.Bass("TRN2", target_bir_lowering=False)
    din = lambda n, s: nc.dram_tensor(n, s, F32, kind="ExternalInput").ap()
    dout = lambda n, s: nc.dram_tensor(n, s, F32, kind="ExternalOutput").ap()
    xp = din("xp", [S, D])
    xs = din("xs", [NS, D])
    cc = din("cc", [5, D])
    ck = [din("ck0", [4, 128, 2, 512]), din("ck1", [4, 512, 2, 512]), din("ck2", [4, 2048, 2, 512])]
    w_cond = din("w_cond", [D, 3 * D])
    b_cond = din("b_cond", [3 * D])
    g_pre = din("g_pre", [D])
    w_in = din("w_in", [D, 10240])
    ln_g = din("ln_v_g", [D])
    ln_b = din("ln_v_b", [D])
    w_sp = din("w_spatial", [8, 128, 128])
    b_sp = din("b_spatial", [8, 128])
    w_pa = din("w_proj_a", [D, D])
    w_pb = din("w_proj_b", [512, D])
    w_out = din("w_out", [D, D])
    g_post = din("g_post", [D])
    yp = dout("yp", [S, D])
    ys = dout("ys", [NS, D])
    kvp = [dout("kvp0", [128, 2, 512]), dout("kvp1", [512, 2, 512]), dout("kvp2", [2048, 2, 512])]
    kvs = [dout("kvs0", [NS, 2, 512]), dout("kvs1", [NS, 2, 512]), dout("kvs2", [NS, 2, 512])]
    vch = dout("vch", [NS, D])

    st = contextlib.ExitStack()
    with st:
        sb = lambda n, s, d: st.enter_context(nc.sbuf_tensor(n, s, d))
        hT = sb("hT", [128, 8, T], BF16)
        ybT = sb("ybT", [128, 4, T], BF16)
        GGp = sb("GGp", [128, D], F32)
        GGs = sb("GGs", [128, D], F32)
        slabs = [sb(f"slab{i}", [128, 8, 512], BF16) for i in range(4)]
        RXN = 41 * 1024
        RYN = 8 * T
        RXt = sb("RX", [128, RXN], BF16)
        RYt = sb("RY", [128, RYN], BF16)
        ident = sb("ident", [128, 128], BF16)
        MKN = sb("MKN", [128, 3, 128], BF16)
        MC = sb("MC", [128, 8], BF16)
        MN = sb("MN", [32, 3, 32], BF16)
        tril = sb("tril", [128, 128], F32)
        ss = sb("ss", [128, 128], F32)
        st2 = sb("st2", [128, 128], F32)
        nhalf = sb("nhalf", [128, 1], F32)
        bnst = [sb(f"bnst{i}", [128, 12], F32) for i in range(3)]
        ps = [st.enter_context(nc.psum_tensor(f"ps{i}", [128, 512], F32)) for i in range(8)]
        RX = Region(RXt, RXN, "RX")
        RY = Region(RYt, RYN, "RY")

        P = Prog(nc)
        psk = [f"ps{i}" for i in range(8)]

        def psb(i):
            return ps[i][:, :].bitcast(BF16)

        def MM(out, lhsT, rhs, start, stop, reads, writes):
            P.op("pe", lambda e: e.matmul(out, lhsT=lhsT, rhs=rhs, start=start, stop=stop, skip_group_check=True),
                 reads, writes)

        def TR(out, in_, idn, reads, writes):
            P.op("pe", lambda e: e.transpose(out=out, in_=in_, identity=idn), reads, writes)

        def ACT(out, in_, func, reads, writes, scale=1.0, bias=None, accum=None):
            def f(e):
                kw = {}
                if bias is not None:
                    kw["bias"] = bias
                if accum is not None:
                    kw["accum_out"] = accum
                return e.activation(out=out, in_=in_, func=func, scale=scale, **kw)
            P.op("act", f, reads, writes)

        def TT(eng, out, in0, in1, op, reads, writes):
            P.op(eng, lambda e: e.tensor_tensor(out=out, in0=in0, in1=in1, op=op), reads, writes)

        def TS(eng, out, in0, s1, s2, op0, op1, reads, writes):
            if s2 is None:
                P.op(eng, lambda e: e.tensor_single_scalar(out=out, in_=in0, scalar=s1, op=op0), reads, writes)
            else:
                P.op(eng, lambda e: e.tensor_scalar(out=out, in0=in0, scalar1=s1, scalar2=s2, op0=op0, op1=op1),
                     reads, writes)

        def STT(eng, out, in0, scalar, in1, op0, op1, reads, writes):
            P.op(eng, lambda e: e.scalar_tensor_tensor(out=out, in0=in0, scalar=scalar, in1=in1, op0=op0, op1=op1),
                 reads, writes)

        def CP(eng, out, in_, reads, writes):
            if eng == "act":
                P.op("act", lambda e: e.activation(out=out, in_=in_, func=AF.Copy), reads, writes)
            else:
                P.op(eng, lambda e: e.tensor_copy(out=out, in_=in_), reads, writes)

        def MS(eng, ap, val, writes):
            P.op(eng, lambda e: e.memset(ap, val), (), writes)

        def ASEL(out, in_, pattern, cmp, fill, base, cm, reads, writes):
            P.op("pool", lambda e: e.affine_select(out=out, in_=in_, pattern=pattern, compare_op=cmp, fill=fill,
                                                   base=base, channel_multiplier=cm), reads, writes)

        dma_id = [0]

        def LD(out, in_, writes, sem=None, reads=(), q="sp", final=False):
            if sem is None:
                dma_id[0] += 1
                sem = f"m{dma_id[0]}"
            P.dma(q, sem, lambda e: e.dma_start(out=out, in_=in_), reads, writes, final=final)

        slab_ctr = [0]

        def take_slabs(n):
            ids = [(slab_ctr[0] + i) % 4 for i in range(n)]
            slab_ctr[0] += n
            return ids

        def load_slab(si, pieces):
            key = f"slab{si}"
            fns = []
            for (off, src) in pieces:
                nk = src.shape[0] // 128
                ncol = src.shape[1]
                fns.append(lambda e, off=off, src=src, nk=nk, ncol=ncol: e.dma_start(
                    out=slabs[si][:, 0:nk, off:off + ncol], in_=src.rearrange("(k p) c -> p k c", p=128)))
            if fns:
                P.dma_group("pool", key, fns, (), [key])

        units = []

        def unit(pieces_per_slab, fn):
            units.append((pieces_per_slab, fn))

        def run_units():
            pend = {}

            def issue(i):
                pieces_per_slab, fn = units[i]
                ids = take_slabs(len(pieces_per_slab))
                for si, pcs in zip(ids, pieces_per_slab):
                    load_slab(si, pcs)
                pend[i] = ids
            for j in range(min(2, len(units))):
                issue(j)
            for i in range(len(units)):
                if i + 2 < len(units):
                    issue(i + 2)
                units[i][1](pend[i])

        MS("pool", nhalf[:], -0.5, ["nhalf"])
        MS("pool", ident[:], 1.0, ["ident"])
        ASEL(ident[:], ident[:], [[-1, 128]], ALU.is_equal, 0.0, 0, 1, ["ident"], ["ident"])
        RX.reset()
        bc_b = RX.alloc([3 * D], F32)
        bc_gpre = RX.alloc([D], F32)
        bc_gpost = RX.alloc([D], F32)
        MA = RX.alloc([2, D], F32)
        MAs = RX.alloc([2, D], F32)
        xt = [RX.alloc([D], F32) for _ in range(3)]
        t1 = [RX.alloc([D], F32) for _ in range(2)]
        hb = [RX.alloc([D], BF16) for _ in range(2)]
        junk = [RX.alloc([D], BF16) for _ in range(2)]
        c_sb = RX.alloc([D], F32)
        c_th = RX.alloc([D], F32)
        c_bf = RX.alloc([D], BF16)
        cTp = RX.alloc([8, 128], BF16)
        cTs = RX.alloc([8, 32], BF16)
        kstgA = [RY.alloc([512], F32) for _ in range(6)]

        LD(c_sb[0:5, :], cc[:, :], ["c_sb"])
        LD(bc_b[:, :], b_cond.partition_broadcast(128), ["bc_b"])
        LD(bc_gpre[:, :], g_pre.partition_broadcast(128), ["bc_gpre"])
        LD(bc_gpost[:, :], g_post.partition_broadcast(128), ["bc_gpost"])
        ACT(c_th[0:5, :], c_sb[0:5, :], AF.Tanh, ["c_sb"], ["c_th"], scale=0.5)
        STT("dve", c_th[0:5, :], c_th[0:5, :], 1.0, c_sb[0:5, :], ALU.add, ALU.mult, ["c_th", "c_sb"], ["c_th"])
        TS("dve", c_bf[0:5, :], c_th[0:5, :], 0.5, None, ALU.mult, None, ["c_th"], ["c_bf"])
        pT7 = psb(7).rearrange("p (k c) -> p k c", c=128)
        for k in range(8):
            TR(pT7[:, k, 0:5], c_bf[0:5, k * 128:(k + 1) * 128], ident[0:5, 0:5], ["c_bf", "ident"], [psk[7]])
        CP("dve", cTp[:, :, :], pT7[:, :, 0:1].broadcast_to([128, 8, 128]), [psk[7]], ["cTp"])
        for b in range(4):
            CP("dve", cTs[:, :, 8 * b:8 * b + 8], pT7[:, :, 1 + b:2 + b].broadcast_to([128, 8, 8]), [psk[7]], ["cTs"])

        def mod_unit(j):
            def fn(ids):
                sl = slabs[ids[0]]
                sk = f"slab{ids[0]}"
                for k in range(8):
                    MM(ps[0][:, :], cTp[:, k, :], sl[:, k, :], k == 0, k == 7, ["cTp", sk], [psk[0]])
                for k in range(8):
                    MM(ps[1][0:32, :], cTs[:, k, :], sl[:, k, :], k == 0, k == 7, ["cTs", sk], [psk[1]])
                cols = slice(512 * j, 512 * j + 512)
                if j < 4:
                    dstp = MA.rearrange("p a b -> p (a b)")[:, cols]
                    dsts = MAs.rearrange("p a b -> p (a b)")[0:32, cols]
                    kp, ks_ = "MA", "MAs"
                else:
                    c2 = slice(512 * (j - 4), 512 * (j - 4) + 512)
                    dstp = GGp[:, c2]
                    dsts = GGs[0:32, c2]
                    kp, ks_ = "GGp", "GGs"
                TT("dve", dstp, ps[0][:, :], bc_b[:, cols], ALU.add, [psk[0], "bc_b"], [kp])
                TT("dve", dsts, ps[1][0:32, :], bc_b[0:32, cols], ALU.add, [psk[1], "bc_b"], [ks_])
            return fn

        for j in range(6):
            unit([[(0, w_cond[:, 512 * j:512 * j + 512])]], mod_unit(j))

        def mod_finish(ids):
            STT("dve", MA[:, 1, :], MA[:, 1, :], 1.0, bc_gpre[:, :], ALU.add, ALU.mult, ["MA", "bc_gpre"], ["MA"])
            STT("dve", MAs[0:32, 1, :], MAs[0:32, 1, :], 1.0, bc_gpre[0:32, :], ALU.add, ALU.mult, ["MAs", "bc_gpre"], ["MAs"])
            TT("pool", GGp[:, :], GGp[:, :], bc_gpost[:, :], ALU.mult, ["GGp", "bc_gpost"], ["GGp"])
            TT("pool", GGs[0:32, :], GGs[0:32, :], bc_gpost[0:32, :], ALU.mult, ["GGs", "bc_gpost"], ["GGs"])
            def s1_A1(i):
                n = 128 if i < 16 else 32
                xb_ = xt[i % 3]
                xk = f"xt{i % 3}"
                src = xp[i * 128:(i + 1) * 128, :] if i < 16 else xs[:, :]
                LD(xb_[0:n, :], src, [xk], sem=f"x{i % 3}")
                jk = f"junk{i % 2}"
                sk1 = f"s1_{i}"
                ACT(junk[i % 2][0:n, :], xb_[0:n, :], AF.Square, [xk], [jk, sk1], accum=ss[0:n, i:i + 1])
                TS("pool", st2[0:n, i:i + 1], ss[0:n, i:i + 1], 1.0 / D, EPS, ALU.mult, ALU.add, [sk1], [sk1])
                TT("pool", st2[0:n, i:i + 1], st2[0:n, i:i + 1], nhalf[0:n, 0:1], ALU.pow, [sk1, "nhalf"], [sk1])

            def s1_A2(i):
                n = 128 if i < 16 else 32
                xb_ = xt[i % 3]
                xk = f"xt{i % 3}"
                sk1 = f"s1_{i}"
                A_ = MA if i < 16 else MAs
                ak = "MA" if i < 16 else "MAs"
                tk1 = f"t1{i % 2}"
                STT("dve", t1[i % 2][0:n, :], xb_[0:n, :], st2[0:n, i:i + 1], A_[0:n, 1, :], ALU.mult, ALU.mult,
                    [xk, sk1, ak], [tk1])
                TT("dve", hb[i % 2][0:n, :], t1[i % 2][0:n, :], A_[0:n, 0, :], ALU.add, [tk1, ak], [f"hb{i % 2}"])

            def s1_B(i):
                n = 128 if i < 16 else 32
                hbi = hb[i % 2]
                hk = f"hb{i % 2}"
                pb = 6 + (i % 2)
                pTv = psb(pb).rearrange("p (k c) -> p k c", c=128)
                for k in range(8):
                    TR(pTv[:, k, 0:n], hbi[0:n, k * 128:(k + 1) * 128], ident[0:n, 0:n], [hk, "ident"], [psk[pb]])
                CP("act", hT[:, :, i * 128:i * 128 + n], pTv[:, :, 0:n], [psk[pb]], [f"hT{i}"])

            def s1_K(i):
                sl = slabs[ids[0]]
                sk = f"slab{ids[0]}"
                n = 128 if i < 16 else 32
                pb = i % 2
                for k in range(8):
                    MM(ps[pb][0:n, :], hT[:, k, i * 128:i * 128 + n], sl[:, k, :], k == 0, k == 7, [f"hT{i}", sk], [psk[pb]])
                stg = kstgA[i % 6]
                skey = f"kstgA{i % 6}"
                CP("act", stg[0:n, :], ps[pb][0:n, :], [psk[pb]], [skey])
                if i < 16:
                    LD(kvp[2][i * 128:i * 128 + 128, 0, :], stg[:, :], [], sem=skey, reads=[skey], final=True, q="act")
                else:
                    LD(kvs[2][:, 0, :], stg[0:32, :], [], sem=skey, reads=[skey], final=True, q="act")

            mgen = make_masks()
            for i in range(23):
                for _ in range(2):
                    next(mgen, None)
                if i < 17:
                    s1_A1(i)
                if 6 <= i < 23:
                    s1_K(i - 6)
                if 1 <= i < 18:
                    s1_A2(i - 1)
                if 2 <= i < 19:
                    s1_B(i - 2)
            for _ in mgen:
                pass

        unit([[(0, w_in[:, 4608 + 1024:4608 + 1536])]], mod_finish)

        def kout_unit(g):
            win, d = PAIRS[g]
            tiles = list(range(16 - win // 128, 16)) + [16]

            def fn(ids):
                sl = slabs[ids[0]]
                sk = f"slab{ids[0]}"
                for ti, i in enumerate(tiles):
                    n = 128 if i < 16 else 32
                    pb = ti % 2
                    for k in range(8):
                        MM(ps[pb][0:n, :], hT[:, k, i * 128:i * 128 + n], sl[:, k, :], k == 0, k == 7, ["hT", sk], [psk[pb]])
                    stg = kstg[ti % 4]
                    skey = f"kstg{ti % 4}"
                    CP("act" if ti % 2 == 0 else "dve", stg[0:n, :], ps[pb][0:n, :], [psk[pb]], [skey])
                    if i < 16:
                        r0 = i * 128 - (S - win)
                        LD(kvp[g][r0:r0 + 128, 0, :], stg[:, :], [], sem=skey, reads=[skey], final=True)
                    else:
                        LD(kvs[g][:, 0, :], stg[0:32, :], [], sem=skey, reads=[skey], final=True)
            return fn

        def barrier(regions):
            for rname in regions:
                c_ = 127 if rname == "RX" else 126
                MS("pool", ss[:, c_:c_ + 1], 0.0, [rname + "_ep"])

        RX.reset()
        kstg = [RX.alloc([512], F32) for _ in range(4)]
        for nm in ("kstg0", "kstg1"):
            P.alias[nm] = "RX_ep"
        qT = [RX.alloc([T], BF16) for _ in range(2)]
        kT = [RX.alloc([T], BF16) for _ in range(2)]
        Vaug = [RX.alloc([17, 2, 128], BF16) for _ in range(2)]
        PT = [RX.alloc([256], BF16) for _ in range(3)]
        Kc = [RX.alloc([8, 128], BF16) for _ in range(4)]
        Vc = [RX.alloc([8, 2, 128], BF16) for _ in range(4)]
        KTs = [RX.alloc([8, 128], BF16) for _ in range(4)]
        PTn = RX.alloc([64], BF16)
        PTc = [RX.alloc([16], BF16) for _ in range(4)]
        vstg = [RX.alloc([128], F32) for _ in range(4)]
        ztmp = RX.alloc([512], F32)
        RY.reset()
        ACC = [RY.alloc([T], F32) for _ in range(2)]
        SZB = [RY.alloc([T], BF16) for _ in range(2)]
        RD = RY.alloc([T], F32)
        s3keys = (["qT0", "qT1", "kT0", "kT1", "Vaug0", "Vaug1", "PT0", "PT1", "PT2", "Kc0", "Kc1", "Vc0", "Vc1",
                   "KTs", "PTn", "PTc", "vstg0", "vstg1", "vstg2", "vstg3", "ztmp"])
        for nm in s3keys:
            P.alias[nm] = "RX_ep"
        for nm in ("ACC0", "ACC1", "SZB", "RD"):
            P.alias[nm] = "RY_ep"
        for nm in ("bc_b", "bc_gpre", "bc_gpost", "MA", "MAs", "xt0", "xt1", "xt2", "t10", "t11", "hb0", "hb1", "junk0", "junk1", "c_sb", "c_th",
                   "c_bf", "cTp", "cTs"):
            P.alias[nm] = "RX_ep"

        def make_masks():
            MS("pool", MKN[:], 0.0, ["MKN"])
            yield
            for idx in (0, 2):
                ASEL(MKN[:, idx, :], MKN[:, idx, :], [[-1, 128]], ALU.is_ge, -30000.0, 0, 1, ["MKN"], ["MKN"])
                yield
            ASEL(MKN[:, 1, :], MKN[:, 1, :], [[1, 128]], ALU.is_ge, -30000.0, 0, -1, ["MKN"], ["MKN"])
            yield
            MS("pool", MC[:], 1.0, ["MC"])
            yield
            ASEL(MC[:], MC[:], [[-1, 8]], ALU.is_ge, 0.0, 0, 1, ["MC"], ["MC"])
            yield
            MS("pool", MN[:], 1.0, ["MN"])
            yield
            ASEL(MN[:, 0, :], MN[:, 0, :], [[1, 32]], ALU.is_ge, 0.0, 0, -1, ["MN"], ["MN"])
            yield
            ASEL(MN[:, 2, :], MN[:, 2, :], [[1, 32]], ALU.is_equal, 0.0, 0, -1, ["MN"], ["MN"])
            yield
            ASEL(MN[:, 1, :], MN[:, 1, :], [[1, 32]], ALU.is_equal, 0.0, -4, -1, ["MN"], ["MN"])
            yield
            for g in range(2):
                for b in range(4):
                    blk = MN[:, g, 8 * b:8 * b + 8]
                    ASEL(blk, blk, [[0, 8]], ALU.is_ge, 0.0, -8 * b, 1, ["MN"], ["MN"])
                    yield
                    ASEL(blk, blk, [[0, 8]], ALU.is_ge, 0.0, 8 * b + 7, -1, ["MN"], ["MN"])
                    yield
            TT("pool", MN[:, 1, :], MN[:, 1, :], MN[:, 2, :], ALU.add, ["MN"], ["MN"])
            yield
            MS("pool", tril[:], 1.0, ["tril"])
            yield
            ASEL(tril[:], tril[:], [[-1, 128]], ALU.is_ge, 0.0, 0, 1, ["tril"], ["tril"])
            yield


        def s3_begin(ids):
            barrier(["RX", "RY"])
            for b in range(2):
                MS("dve", Vaug[b][:, :, :, :], 1.0, [f"Vaug{b}"])
            for b in range(4):
                MS("dve", Vc[b][:, :, :, :], 1.0, [f"Vc{b}"])

        unit([], s3_begin)
        for g in range(2):
            unit([[(0, w_in[:, 4608 + 512 * g:4608 + 512 * g + 512])]], kout_unit(g))

        st_ring = [2, 3, 7]
        pj_ctr = [0]

        def next_pj():
            pj_ctr[0] += 1
            return pj_ctr[0] % 2

        def w_pieces(u):
            p, g = divmod(u, 3)
            pcs = [(0, w_in[:, 3072 + 512 * g + 128 * p:3072 + 512 * g + 128 * p + 128]),
                   (128, w_in[:, 4608 + 512 * g + 128 * p:4608 + 512 * g + 128 * p + 128]),
                   (256, w_in[:, 6144 + 512 * g + 128 * p:6144 + 512 * g + 128 * p + 128])]
            if g == 0:
                pcs.append((384, w_in[:, 7680 + 128 * p:7680 + 128 * p + 128]))
            return pcs

        def mk_tok(d):
            def tok(r, n, cnt):
                s0 = d * 128 * n + r
                return slice(s0, s0 + d * (cnt - 1) + 1, d) if d > 1 else slice(s0, s0 + cnt)
            return tok

        def proj_gen(u, sl, sk):
            p, g = divmod(u, 3)
            win, d = PAIRS[g]
            nb = 16 // d
            tok = mk_tok(d)
            ub = u % 2
            qk_, kk_, vk_ = f"qT{ub}", f"kT{ub}", f"Vaug{ub}"
            q_, k_, V_ = qT[ub], kT[ub], Vaug[ub]
            szb, szk = SZB[p % 2], f"SZB{p % 2}"

            def fm(coff, consume):
                for (c0, cn) in CH:
                    pb = next_pj()
                    for k in range(8):
                        MM(ps[pb][:, 0:cn], sl[:, k, coff:coff + 128], hT[:, k, c0:c0 + cn], k == 0, k == 7,
                           [sk, "hT"], [psk[pb]])
                    consume(pb, c0, cn)
                    yield

            if g == 0:
                def cons_z(pb, c0, cn):
                    ACT(ztmp[:, 0:cn], ps[pb][:, 0:cn], AF.Tanh, [psk[pb]], ["ztmp"], scale=0.5)
                    STT("dve", szb[:, c0:c0 + cn], ztmp[:, 0:cn], 1.0, ps[pb][:, 0:cn], ALU.add, ALU.mult,
                        ["ztmp", psk[pb]], [szk])
                yield from fm(384, cons_z)

            def perm_store(dst, key, pb, c0, cn):
                if d > 1 and cn == 512:
                    o = dst[:, 0:S].rearrange("p (r m) -> p r m", r=d)[:, :, c0 // d:(c0 + cn) // d]
                    i_ = ps[pb][:, 0:cn].rearrange("p (m r) -> p r m", r=d)
                    CP("act", o, i_, [psk[pb]], [key])
                else:
                    CP("act", dst[:, c0:c0 + cn], ps[pb][:, 0:cn], [psk[pb]], [key])

            def cons_q(pb, c0, cn):
                perm_store(q_, qk_, pb, c0, cn)

            def cons_k(pb, c0, cn):
                perm_store(k_, kk_, pb, c0, cn)
            yield from fm(0, cons_q)
            yield from fm(128, cons_k)
            blocks = [(r, n) for r in range(d) for n in range(nb)]
            for b0 in range(0, 16, 4):
                pb = next_pj()
                pv = ps[pb][:, :].rearrange("p (s c) -> p s c", c=128)
                for s_ in range(4):
                    r, n = blocks[b0 + s_]
                    for k in range(8):
                        MM(pv[:, s_, :], hT[:, k, tok(r, n, 128)], sl[:, k, 256:384], k == 0, k == 7,
                           ["hT", sk], [psk[pb]])
                CP("dve", V_[:, b0:b0 + 4, 0, 0:64], pv[:, :, 0:64], [psk[pb]], [vk_])
                CP("dve", V_[:, b0:b0 + 4, 1, 64:128], pv[:, :, 64:128], [psk[pb]], [vk_])
                for s_ in range(4):
                    r, n = blocks[b0 + s_]
                    if n == nb - 1:
                        vi = (b0 + s_) % 4
                        CP("dve", vstg[vi][:, :], pv[:, s_, :], [psk[pb]], [f"vstg{vi}"])
                        r0 = d * 128 * n + r - (S - win)
                        dst = kvp[g][r0:r0 + d * 127 + 1:d, 1, p * 128:(p + 1) * 128] if d > 1 else \
                            kvp[g][r0:r0 + 128, 1, p * 128:(p + 1) * 128]
                        LD(dst, vstg[vi][:, :], [], sem=f"vstg{vi}", reads=[f"vstg{vi}"], final=True)
                yield
            pb = next_pj()
            for k in range(8):
                MM(ps[pb][0:32, 0:128], hT[:, k, S:T], sl[:, k, 256:384], k == 0, k == 7, ["hT", sk], [psk[pb]])
            CP("dve", V_[0:32, 16, 0, 0:64], ps[pb][0:32, 0:64], [psk[pb]], [vk_])
            CP("dve", V_[0:32, 16, 1, 64:128], ps[pb][0:32, 64:128], [psk[pb]], [vk_])
            CP("dve", vstg[0][0:32, :], ps[pb][0:32, 0:128], [psk[pb]], ["vstg0"])
            LD(kvs[g][:, 1, p * 128:(p + 1) * 128], vstg[0][0:32, :], [], sem="vstg0", reads=["vstg0"], final=True)
            yield

        def attend_gen(u):
            p, g = divmod(u, 3)
            win, d = PAIRS[g]
            nb = 16 // d
            tok = mk_tok(d)
            ub = u % 2
            qk_, kk_, vk_ = f"qT{ub}", f"kT{ub}", f"Vaug{ub}"
            q_, k_, V_ = qT[ub], kT[ub], Vaug[ub]
            szb, szk = SZB[p % 2], f"SZB{p % 2}"
            nres = (1, 4, 8)[g]
            nq = (8, 2, 1)[g]
            osb = 6
            for b in range(4):
                kck, vck = f"Kc{b}", f"Vc{b}"
                csrc = ck[g][b].rearrange("(m r) t f -> m r t f", r=d)
                P.dma("pool", kck, lambda e, b=b, csrc=csrc: e.dma_start(
                    out=Kc[b][:, 0:nres, :], in_=csrc[:, 0:nres, 0, p * 128:(p + 1) * 128]), (), [kck])
                P.dma_group("pool", vck, [lambda e, b=b, csrc=csrc, e2=e2: e.dma_start(
                    out=Vc[b][:, 0:nres, e2, 64 * e2:64 * e2 + 64],
                    in_=csrc[:, 0:nres, 1, p * 128 + 64 * e2:p * 128 + 64 * e2 + 64]) for e2 in range(2)], (), [vck])
            if g < 2:
                rounds = [[(r, n0 + s_) for s_ in range(4)] for r in range(d) for n0 in range(0, nb, 4)]
            else:
                rounds = [[(r0 + s_, 0) for s_ in range(4)] for r0 in range(0, 16, 4)]
            items = []
            for rnd in rounds:
                for e in range(2):
                    sub = []
                    if g < 2:
                        r, n0 = rnd[0]
                        if n0 >= 1:
                            sub.append(((r, n0 - 1), 0, 1, MKN[:, 0, :]))
                        for s_ in range(4):
                            if s_ < 3:
                                sub.append(((r, n0 + s_), s_, 2, MKN[:, 1:3, :].rearrange("p a b -> p (a b)")))
                            else:
                                sub.append(((r, n0 + s_), s_, 1, MKN[:, 1, :]))
                    else:
                        for s_ in range(4):
                            sub.append((rnd[s_], s_, 1, MKN[:, 1, :]))
                    for ii, (kb, s0, nsl, msk) in enumerate(sub):
                        items.append(dict(rnd=rnd, e=e, kb=kb, s0=s0, nsl=nsl, msk=msk, first=(ii == 0),
                                          last=(ii == len(sub) - 1)))

            def emit_st(it, i):
                e = it["e"]
                hs = slice(64 * e, 64 * e + 64)
                N = 128 * it["nsl"]
                sbk = st_ring[i % 3]
                pti = i % 3
                kr, kn = it["kb"]
                qr, qn = it["rnd"][it["s0"]]
                LL = S // d
                MM(ps[sbk][:, 0:N], k_[hs, kr * LL + 128 * kn:kr * LL + 128 * kn + 128],
                   q_[hs, qr * LL + 128 * qn:qr * LL + 128 * qn + N], True, False, [kk_, qk_], [psk[sbk]])
                MM(ps[sbk][:, 0:N], ident[:, :], it["msk"], False, True, ["ident", "MKN"], [psk[sbk]])
                ACT(PT[pti][:, 0:N], ps[sbk][:, 0:N], AF.Exp, [psk[sbk]], [f"PT{pti}"], scale=0.125)

            def emit_pv(it, i):
                e = it["e"]
                ob = 4 + e
                N = 128 * it["nsl"]
                pti = i % 3
                kr, kn = it["kb"]
                s0 = it["s0"]
                rnd = it["rnd"]
                MM(ps[ob][:, 128 * s0:128 * s0 + N], V_[:, kr * nb + kn, e, :], PT[pti][:, 0:N], it["first"], False,
                   [vk_, f"PT{pti}"], [psk[ob]])
                if it["last"]:
                    if g == 0:
                        dst = ACC[e][:, 128 * rnd[0][1]:128 * rnd[0][1] + 512]
                        src = ps[ob][:, :]
                    elif g == 1:
                        r = rnd[0][0]
                        dst = ACC[e][:, r:S:4]
                        src = ps[ob][:, :]
                    else:
                        r0 = rnd[0][0]
                        dst = ACC[e][:, 0:S].rearrange("p (j r) -> p r j", r=16)[:, r0:r0 + 4, :]
                        src = ps[ob][:, :].rearrange("p (s c) -> p s c", c=128)
                    if g == 0:
                        CP("dve", dst, src, [psk[ob]], [f"ACC{e}"])
                    else:
                        TT("dve", dst, src, dst, ALU.add, [psk[ob], f"ACC{e}"], [f"ACC{e}"])

            SK = 2
            for i in range(len(items) + SK):
                if i < len(items):
                    emit_st(items[i], i)
                if i - SK >= 0:
                    emit_pv(items[i - SK], i - SK)
                yield

            def finalize(cs_):
                ACT(RD[0:64, cs_], ACC[0][64:128, cs_], AF.Ln, ["ACC0"], ["RD"])
                ACT(RD[64:128, cs_], ACC[1][0:64, cs_], AF.Ln, ["ACC1"], ["RD"])
                ACT(RD[:, cs_], RD[:, cs_], AF.Exp, ["RD"], ["RD"], scale=-1.0)
                STT("dve", RD[:, cs_], RD[:, cs_], 0.5, szb[:, cs_], ALU.mult, ALU.mult, ["RD", szk], ["RD"])
                TT("dve", ybT[0:64, p, cs_], ACC[0][0:64, cs_], RD[0:64, cs_], ALU.mult, ["ACC0", "RD"], ["ybT"])
                TT("pool", ybT[64:128, p, cs_], ACC[1][64:128, cs_], RD[64:128, cs_], ALU.mult, ["ACC1", "RD"], ["ybT"])

            sbk = st_ring[0]
            for e in range(2):
                hs = slice(64 * e, 64 * e + 64)
                MM(ps[sbk][0:32, 32 * e:32 * e + 32], k_[hs, S:T], q_[hs, S:T], True, True, [kk_, qk_],
                   [psk[sbk], "pe_ser"])
            ACT(PTn[0:32, :], ps[sbk][0:32, 0:64], AF.Exp, [psk[sbk]], ["PTn"], scale=0.125)
            mn = MN[:, g, :]
            mn2 = bass.AP(tensor=mn.tensor, offset=mn.offset, ap=[list(mn.ap[0]), [0, 2], list(mn.ap[1])])
            TT("dve", PTn[0:32, :].rearrange("p (e q) -> p e q", e=2), PTn[0:32, :].rearrange("p (e q) -> p e q", e=2),
               mn2, ALU.mult, ["PTn", "MN"], ["PTn"])
            for e in range(2):
                MM(ps[osb][:, 32 * e:32 * e + 32], V_[0:32, 16, e, :], PTn[0:32, 32 * e:32 * e + 32],
                   (g == 0 and e == 0), False, [vk_, "PTn"], [psk[osb]])
            yield
            for b in range(4):
                pb = next_pj()
                pTv = psb(pb).rearrange("p (k c) -> p k c", c=128)
                for r in range(nres):
                    TR(pTv[:, r, :], Kc[b][:, r, :], ident[:, :], [f"Kc{b}", "ident"], [psk[pb]])
                CP("dve", KTs[b][:, 0:nres, :], pTv[:, 0:nres, :], [psk[pb]], [f"KTs{b}"])
                yield
            for b in range(4):
                sbk = st_ring[(b + 1) % 3]
                stv = ps[sbk][:, 0:16].rearrange("p (r e q) -> p r e q", e=2, q=nq)
                for e in range(2):
                    for r in range(nres):
                        hs = slice(64 * e, 64 * e + 64)
                        q0 = S + 8 * b + (r if g > 0 else 0)
                        qs = slice(q0, q0 + 5, 4) if g == 1 else slice(q0, q0 + nq)
                        ser = ["pe_ser"] if (r == nres - 1 and e == 0) or (r == 0 and e == 1) else []
                        MM(stv[:, r, e, :], KTs[b][hs, r, :], q_[hs, qs], True, True, [f"KTs{b}", qk_], [psk[sbk]] + ser)
                ACT(PTc[b][:, :], ps[sbk][:, 0:16], AF.Exp, [psk[sbk]], [f"PTc{b}"], scale=0.125)
                mc = bass.AP(tensor=MC, offset=0, ap=[[8, 128], [0, nres * 2], [1, nq]])
                ptv = PTc[b][:, :].rearrange("p (a q) -> p a q", q=nq)
                TT("dve", ptv, ptv, mc, ALU.mult, [f"PTc{b}", "MC"], [f"PTc{b}"])
                yield
            for b in range(4):
                ptv4 = PTc[b][:, :].rearrange("p (r e q) -> p r e q", e=2, q=nq)
                for r in range(nres):
                    for e in range(2):
                        o0 = 32 * e + 8 * b + (r if g > 0 else 0)
                        osl = slice(o0, o0 + 5, 4) if g == 1 else slice(o0, o0 + nq)
                        MM(ps[osb][:, osl], Vc[b][:, r, e, :], ptv4[:, r, e, :], False, False, [f"Vc{b}", f"PTc{b}"], [psk[osb]])
                yield
            if g == 2:
                for e in range(2):
                    CP("act", ACC[e][:, S:T], ps[osb][:, 32 * e:32 * e + 32], [psk[osb]], [f"ACC{e}"])
                finalize(slice(0, T))
                yield

        def run_interleaved(a, b, rates):
            done_b = b is None
            acc = 0.0
            for i, _ in enumerate(a):
                if not done_b:
                    acc += rates[i] if i < len(rates) else 1.0
                    while acc >= 1.0 and not done_b:
                        acc -= 1.0
                        try:
                            next(b)
                        except StopIteration:
                            done_b = True
            if not done_b:
                for _ in b:
                    pass

        NU = 12

        def attn_pre(ids):
            for _ in proj_gen(0, slabs[ids[0]], f"slab{ids[0]}"):
                pass

        def attn_unit(u):
            def fn(ids):
                nxt = proj_gen(u + 1, slabs[ids[0]], f"slab{ids[0]}") if u + 1 < NU else None
                g = u % 3
                npr = (40 if g == 0 else 32) + 2
                nsm = 13 + (1 if g == 2 else 0)
                nbg = 20 if (u + 1) % 3 == 0 else 15
                r2 = (0.75, 0.55, 0.4)[g]
                r1 = max(nbg - r2 * (nsm - 3), 0.0) / npr
                run_interleaved(attend_gen(u), nxt, [r1] * npr + [r2] * nsm)
            return fn

        unit([w_pieces(0)], attn_pre)
        for u in range(NU):
            unit([w_pieces(u + 1)] if u + 1 < NU else [], attn_unit(u))

        RX.reset()
        vn = RX.alloc([17, D], BF16)
        lng = RX.alloc([D], F32)
        lnb = RX.alloc([D], F32)
        va = [RX.alloc([D], F32) for _ in range(3)]
        va2 = RX.alloc([D], BF16)
        bsp = RX.alloc([8, 128], F32)
        wspT = RX.alloc([8, 128], BF16)
        wspTs = RX.alloc([8, 32], BF16)
        wspn = RX.alloc([8, 128], F32)
        wspb = RX.alloc([8, 128], BF16)
        wsps = RX.alloc([8, 32], F32)
        wspsb = RX.alloc([8, 32], BF16)
        tmpA = [[RX.alloc([512], F32) for _ in range(3)] for _ in range(2)]
        hbsp = bsp
        RY.reset()
        yaT = RY.alloc([8, T], BF16)
        for nm in (["vn", "lng", "lnb", "va0", "va1", "va2", "vb2", "bsp", "hbsp", "wspT", "wspTs", "wspn", "wspb", "wsps", "wspsb"]
                   + [f"tA{i}{j}" for i in range(2) for j in range(5)]):
            P.alias[nm] = "RX_ep"
        P.alias["yaT"] = "RY_ep"

        def s2_begin(ids):
            barrier(["RX", "RY"])
            LD(lng[:, :], ln_g.partition_broadcast(128), ["lng"])
            LD(lnb[:, :], ln_b.partition_broadcast(128), ["lnb"])
            LD(bsp[:, :, :].rearrange("p g t -> p (g t)"), b_sp.rearrange("g t -> (g t)").partition_broadcast(128), ["bsp"])
            LD(wspn[:, :, :], w_sp.rearrange("g t s -> t g s"), ["wspn"])
            MS("pool", wsps[0:32, :, :], 0.0, ["wsps"])
            for b in range(4):
                LD(wsps[8 * b:8 * b + 8, :, 8 * b:8 * b + 8], w_sp[:, 0:8, 0:8].rearrange("g t s -> t g s"), ["wsps"])

        def wsp_prep():
            TS("dve", bsp[:, :, :], bsp[:, :, :], 0.5, None, ALU.mult, None, ["bsp"], ["bsp"])
            tri3 = bass.AP(tensor=tril.tensor if hasattr(tril, "tensor") else tril, offset=0, ap=[[128, 128], [0, 8], [1, 128]])
            TT("dve", wspb[:, :, :], wspn[:, :, :], tri3, ALU.mult, ["wspn", "tril"], ["wspb"])
            pTv = psb(6).rearrange("p (k c) -> p k c", c=128)
            for g8 in range(8):
                TR(pTv[:, g8, :], wspb[:, g8, :], ident[:, :], ["wspb", "ident"], [psk[6]])
            CP("dve", wspT[:, :, :], pTv[:, :, :], [psk[6]], ["wspT"])
            tri3s = bass.AP(tensor=tril.tensor if hasattr(tril, "tensor") else tril, offset=0, ap=[[128, 32], [0, 8], [1, 32]])
            TT("dve", wspsb[0:32, :, :], wsps[0:32, :, :], tri3s, ALU.mult, ["wsps", "tril"], ["wspsb"])
            pTs = psb(7).rearrange("p (k c) -> p k c", c=128)
            for g8 in range(8):
                TR(pTs[0:32, g8, 0:32], wspsb[0:32, g8, :], ident[0:32, 0:32], ["wspsb", "ident"], [psk[7]])
            CP("dve", wspTs[0:32, :, :], pTs[0:32, :, 0:32], [psk[7]], ["wspTs"])

        unit([], s2_begin)

        def va_unit(ids):
            def ph1(i):
                n = 128 if i < 16 else 32
                pb0 = 2 * (i % 3)
                for h in range(2):
                    sl = slabs[ids[h]]
                    sk = f"slab{ids[h]}"
                    for k in range(8):
                        MM(ps[pb0 + h][0:n, :], hT[:, k, i * 128:i * 128 + n], sl[:, k, :], k == 0, k == 7,
                           ["hT", sk], [psk[pb0 + h]])
                v_ = va[i % 3]
                vk = f"va{i % 3}"
                c0 = 20 + 2 * i
                lk = f"ln_{i}"
                bst = bnst[i % 3]
                for h in range(2):
                    P.op("dve", lambda e, h=h, bst=bst, n=n, pb0=pb0: e.bn_stats(out=bst[0:n, 6 * h:6 * h + 6], in_=ps[pb0 + h][0:n, :]),
                         [psk[pb0 + h]], [lk])
                P.op("dve", lambda e, bst=bst, n=n, c0=c0: e.bn_aggr(out=ss[0:n, c0:c0 + 2], in_=bst[0:n, 0:12]), [lk], [lk])
                TS("dve", st2[0:n, c0:c0 + 1], ss[0:n, c0 + 1:c0 + 2], EPS, None, ALU.add, None, [lk], [lk])
                TT("pool", st2[0:n, c0:c0 + 1], st2[0:n, c0:c0 + 1], nhalf[0:n, 0:1], ALU.pow, [lk, "nhalf"], [lk])
                STT("dve", st2[0:n, c0 + 1:c0 + 2], ss[0:n, c0:c0 + 1], -1.0, st2[0:n, c0:c0 + 1], ALU.mult, ALU.mult, [lk], [lk])
                for h in range(2):
                    ACT(v_[0:n, 512 * h:512 * h + 512], ps[pb0 + h][0:n, :], AF.Identity, [psk[pb0 + h], lk], [vk],
                        scale=st2[0:n, c0:c0 + 1], bias=st2[0:n, c0 + 1:c0 + 2])

            def ph2(i):
                n = 128 if i < 16 else 32
                v_ = va[i % 3]
                vk = f"va{i % 3}"
                c0 = 20 + 2 * i
                lk = f"ln_{i}"
                TT("dve", v_[0:n, :], v_[0:n, :], lng[0:n, :], ALU.mult, [vk, "lng"], [vk])
                if i < 16:
                    TT("pool" if i % 2 == 0 else "dve", vn[0:n, i, :], v_[0:n, :], lnb[0:n, :], ALU.add, [vk, "lnb"], [f"vn{i}"])
                else:
                    TT("pool", v_[0:n, :], v_[0:n, :], lnb[0:n, :], ALU.add, [vk, "lnb"], [vk])
                    CP("dve", vn[0:n, i, :], v_[0:n, :], [vk], [f"vn{i}"])
                    LD(vch[:, :], v_[0:32, :], [], sem="vch", reads=[vk], final=True)

            for i in range(18):
                if i < 17:
                    ph1(i)
                if i >= 1:
                    ph2(i - 1)
                if i == 12:
                    wsp_prep()

        unit([[(0, w_in[:, 1024:1536])], [(0, w_in[:, 1536:2048])]], va_unit)

        def ya_unit(g8):
            def fn(ids):
                sl = slabs[ids[0]]
                sk = f"slab{ids[0]}"
                for ci, (c0, cn) in enumerate(CH):
                    b3 = 3 * (ci % 2)
                    U, Z, ZS = b3, b3 + 1, b3 + 2
                    for k in range(8):
                        MM(ps[U][:, 0:cn], sl[:, k, 0:128], hT[:, k, c0:c0 + cn], k == 0, k == 7, [sk, "hT"], [psk[U]])
                    for k in range(8):
                        MM(ps[Z][:, 0:cn], sl[:, k, 128:256], hT[:, k, c0:c0 + cn], k == 0, k == 7, [sk, "hT"], [psk[Z]])
                    if cn == 512:
                        for s_ in range(4):
                            i = c0 // 128 + s_
                            MM(ps[ZS][:, 128 * s_:128 * s_ + 128], vn[:, i, g8 * 128:(g8 + 1) * 128], wspT[:, g8, :],
                               True, True, ["vn", "wspT"], [psk[ZS]])
                        bias = bass.AP(tensor=hbsp.tensor, offset=hbsp[:, g8, :].offset, ap=[list(hbsp.ap[0]), [0, 4], [1, 128]])
                        zs_in = ps[ZS][:, :].rearrange("p (s c) -> p s c", c=128)
                    else:
                        MM(ps[ZS][:, 0:32], vn[0:32, 16, g8 * 128:(g8 + 1) * 128], wspTs[0:32, g8, :], True, True,
                           ["vn", "wspTs"], [psk[ZS]])
                        bias = bass.AP(tensor=hbsp.tensor, offset=hbsp[:, g8, :].offset, ap=[list(hbsp.ap[0]), [0, 4], [1, 8]])
                        zs_in = ps[ZS][:, 0:32].rearrange("p (s c) -> p s c", c=8)
                    tt_ = tmpA[ci % 2]
                    tk = [f"tA{ci % 2}{j}" for j in range(3)]
                    cs = cn // 4
                    ACT(tt_[0][:, 0:cn], ps[Z][:, 0:cn], AF.Tanh, [psk[Z]], [tk[0]], scale=0.5)
                    STT("dve", tt_[0][:, 0:cn], tt_[0][:, 0:cn], 1.0, ps[Z][:, 0:cn], ALU.add, ALU.mult, [tk[0], psk[Z]], [tk[0]])
                    STT("dve", tt_[1][:, 0:cn].rearrange("p (s c) -> p s c", c=cs), zs_in, 0.5, bias, ALU.mult, ALU.add,
                        [psk[ZS], "bsp"], [tk[1]])
                    TT("dve", tt_[1][:, 0:cn], tt_[1][:, 0:cn], ps[U][:, 0:cn], ALU.mult, [tk[1], psk[U]], [tk[1]])
                    TT("pool", yaT[:, g8, c0:c0 + cn], tt_[1][:, 0:cn], tt_[0][:, 0:cn], ALU.mult, [tk[1], tk[0]], ["yaT"])
            return fn

        for g8 in range(8):
            unit([[(0, w_in[:, 128 * g8:128 * g8 + 128]), (128, w_in[:, 2048 + 128 * g8:2048 + 128 * g8 + 128])]], ya_unit(g8))

        RX4 = Region(RXt, RXN, "RX4")
        mgT = RX4.alloc([8, T], BF16)
        tmpM = [[RX4.alloc([512], F32) for _ in range(6)] for _ in range(2)]
        xt4 = [RX4.alloc([D], F32) for _ in range(2)]
        ot4 = [RX4.alloc([D], F32) for _ in range(2)]
        sq4 = RX4.alloc([512], F32)
        for nm in (["mgT", "x40", "x41", "o40", "o41", "sq4"] + [f"tM{i}{j}" for i in range(2) for j in range(6)]):
            P.alias[nm] = "RX_ep"

        def s4_begin(ids):
            barrier(["RX"])

        unit([], s4_begin)

        def mg_unit(j):
            def fn(ids):
                sl = slabs[ids[0]]
                sk = f"slab{ids[0]}"
                for ci, (c0, cn) in enumerate(CH):
                    b4 = 4 * (ci % 2)
                    PA, PB, GA, GB = b4, b4 + 1, b4 + 2, b4 + 3
                    for k in range(8):
                        MM(ps[PA][:, 0:cn], sl[:, k, 0:128], yaT[:, k, c0:c0 + cn], k == 0, k == 7, [sk, "yaT"], [psk[PA]])
                    for k in range(4):
                        MM(ps[PB][:, 0:cn], sl[:, k, 128:256], ybT[:, k, c0:c0 + cn], k == 0, k == 3, [sk, "ybT"], [psk[PB]])
                    for k in range(8):
                        MM(ps[GA][:, 0:cn], sl[:, k, 256:384], hT[:, k, c0:c0 + cn], k == 0, k == 7, [sk, "hT"], [psk[GA]])
                    for k in range(8):
                        MM(ps[GB][:, 0:cn], sl[:, k, 384:512], hT[:, k, c0:c0 + cn], k == 0, k == 7, [sk, "hT"], [psk[GB]])
                    tt_ = tmpM[ci % 2]
                    tk = [f"tM{ci % 2}{q}" for q in range(6)]
                    ACT(tt_[0][:, 0:cn], ps[GA][:, 0:cn], AF.Tanh, [psk[GA]], [tk[0]], scale=0.5)
                    ACT(tt_[1][:, 0:cn], ps[GB][:, 0:cn], AF.Tanh, [psk[GB]], [tk[1]], scale=0.5)
                    TS("pool", tt_[2][:, 0:cn], tt_[0][:, 0:cn], 0.5, 0.5, ALU.mult, ALU.add, [tk[0]], [tk[2]])
                    TS("pool", tt_[3][:, 0:cn], tt_[1][:, 0:cn], 0.5, 0.5, ALU.mult, ALU.add, [tk[1]], [tk[3]])
                    TT("dve", tt_[4][:, 0:cn], ps[PA][:, 0:cn], tt_[2][:, 0:cn], ALU.mult, [psk[PA], tk[2]], [tk[4]])
                    TT("dve", tt_[5][:, 0:cn], ps[PB][:, 0:cn], tt_[3][:, 0:cn], ALU.mult, [psk[PB], tk[3]], [tk[5]])
                    TT("pool", mgT[:, j, c0:c0 + cn], tt_[4][:, 0:cn], tt_[5][:, 0:cn], ALU.add, [tk[4], tk[5]], ["mgT"])
            return fn

        for j in range(8):
            unit([[(0, w_pa[:, 128 * j:128 * j + 128]), (128, w_pb[:, 128 * j:128 * j + 128]),
                   (256, w_in[:, 8192 + 128 * j:8192 + 128 * j + 128]),
                   (384, w_in[:, 9216 + 128 * j:9216 + 128 * j + 128])]], mg_unit(j))

        def out_unit(ids):
            def ph1(i):
                n = 128 if i < 16 else 32
                pb0 = 2 * (i % 3)
                xk = f"x4{i % 2}"
                src = xp[i * 128:(i + 1) * 128, :] if i < 16 else xs[:, :]
                LD(xt4[i % 2][0:n, :], src, [xk], sem=xk)
                for h in range(2):
                    sl = slabs[ids[h]]
                    sk = f"slab{ids[h]}"
                    for k in range(8):
                        MM(ps[pb0 + h][0:n, :], mgT[:, k, i * 128:i * 128 + n], sl[:, k, :], k == 0, k == 7,
                           ["mgT", sk], [psk[pb0 + h]])
                c0 = 64 + 2 * i
                fk = f"fst_{i}"
                for h in range(2):
                    ACT(sq4[0:n, :], ps[pb0 + h][0:n, :], AF.Square, [psk[pb0 + h]], ["sq4", fk], accum=ss[0:n, c0 + h:c0 + h + 1])
                TT("dve", st2[0:n, c0:c0 + 1], ss[0:n, c0:c0 + 1], ss[0:n, c0 + 1:c0 + 2], ALU.add, [fk], [fk])
                TS("dve", st2[0:n, c0:c0 + 1], st2[0:n, c0:c0 + 1], 1.0 / D, EPS, ALU.mult, ALU.add, [fk], [fk])
                TT("pool", st2[0:n, c0:c0 + 1], st2[0:n, c0:c0 + 1], nhalf[0:n, 0:1], ALU.pow, [fk, "nhalf"], [fk])

            def ph2(i):
                n = 128 if i < 16 else 32
                pb0 = 2 * (i % 3)
                xk, ok = f"x4{i % 2}", f"o4{i % 2}"
                c0 = 64 + 2 * i
                fk = f"fst_{i}"
                G_ = GGp if i < 16 else GGs
                gk = "GGp" if i < 16 else "GGs"
                o_ = ot4[i % 2]
                for h in range(2):
                    cs = slice(512 * h, 512 * h + 512)
                    STT("dve", o_[0:n, cs], ps[pb0 + h][0:n, :], st2[0:n, c0:c0 + 1], G_[0:n, cs], ALU.mult, ALU.mult,
                        [psk[pb0 + h], fk, gk], [ok])
                TT("pool" if i % 2 == 0 else "dve", o_[0:n, :], o_[0:n, :], xt4[i % 2][0:n, :], ALU.add, [ok, xk], [ok])
                dst = yp[i * 128:(i + 1) * 128, :] if i < 16 else ys[:, :]
                LD(dst, o_[0:n, :], [], sem=ok, reads=[ok], final=True)

            for i in range(18):
                if i < 17:
                    ph1(i)
                if i >= 1:
                    ph2(i - 1)

        unit([[(0, w_out[:, 0:512])], [(0, w_out[:, 512:1024])]], out_unit)

        run_units()
        P.emit()
    return nc


_CACHE = {}


def kernel(x_prompt, x_sample, cache_kv_w128, cache_kv_w512, cache_kv_w2048, c_prompt, c_sample,
           w_cond, b_cond, g_pre, w_in, ln_v_g, ln_v_b, w_spatial, b_spatial,
           w_proj_a, w_proj_b, w_out, g_post):
    f = lambda a: np.ascontiguousarray(np.asarray(a, dtype=np.float32))
    if "nc" not in _CACHE:
        _CACHE["nc"] = build_nc()
    nc = _CACHE["nc"]
    x_prompt, x_sample = f(x_prompt), f(x_sample)
    caches = [f(cache_kv_w128), f(cache_kv_w512), f(cache_kv_w2048)]
    c_prompt, c_sample = f(c_prompt), f(c_sample)
    shared = {
        "w_cond": f(w_cond)[0], "b_cond": f(b_cond)[0], "g_pre": f(g_pre)[0], "w_in": f(w_in)[0],
        "ln_v_g": f(ln_v_g)[0], "ln_v_b": f(ln_v_b)[0], "w_spatial": f(w_spatial)[0], "b_spatial": f(b_spatial)[0],
        "w_proj_a": f(w_proj_a)[0], "w_proj_b": f(w_proj_b)[0], "w_out": f(w_out)[0], "g_post": f(g_post)[0],
    }
    in_maps = []
    for i in range(8):
        m = dict(shared)
        m["xp"] = x_prompt[i]
        m["xs"] = x_sample[4 * i:4 * i + 4].reshape(NS, D)
        m["cc"] = np.concatenate([c_prompt[i:i + 1], c_sample[4 * i:4 * i + 4]], axis=0)
        for g in range(3):
            c = caches[g][0, 4 * i:4 * i + 4]
            m[f"ck{g}"] = np.ascontiguousarray(c.reshape(4, c.shape[1], 2, 512))
        in_maps.append(m)
    res = run_bass_kernel_spmd(nc, in_maps, core_ids=list(range(8)))
    R = res.results
    y_p = np.stack([R[i]["yp"] for i in range(8)], axis=0)
    y_s = np.concatenate([R[i]["ys"].reshape(4, 8, D) for i in range(8)], axis=0)
    outs = [y_p, y_s]
    for g in range(3):
        win = PAIRS[g][0]
        outs.append(np.stack([R[i][f"kvp{g}"].reshape(win, 2, 8, 64) for i in range(8)], axis=0)[None])
    for g in range(3):
        outs.append(np.concatenate([R[i][f"kvs{g}"].reshape(4, 8, 2, 8, 64) for i in range(8)], axis=0)[None])
    outs.append(np.concatenate([R[i]["vch"].reshape(4, 8, D) for i in range(8)], axis=0)[None])
    return tuple(np.ascontiguousarray(o.astype(np.float32)) for o in outs)
```

```python
import contextlib
import numpy as np
import concourse.bass as bass
import concourse.mybir as mybir
from concourse.bass_utils import run_bass_kernel_spmd

F32 = mybir.dt.float32
BF16 = mybir.dt.bfloat16
AF = mybir.ActivationFunctionType
ALU = mybir.AluOpType

D = 1024
S = 2048
NS = 32
T = S + NS
EPS = 1e-6
PAIRS = ((128, 1), (512, 4), (2048, 16))
CH = [(0, 512), (512, 512), (1024, 512), (1536, 512), (2048, 32)]
COMPUTE = ("pe", "act", "dve", "pool")


RXKEYS = set([f"vn{i}" for i in range(17)] + ["bc_b", "bc_gpre", "bc_gpost", "MA", "MAs", "xt0", "xt1", "xt2", "t10", "t11", "hb0", "hb1", "junk0", "junk1", "c_sb", "c_th",
              "c_bf", "cTp", "cTs", "kstg0", "kstg1", "kstg2", "kstg3",
              "qT0", "qT1", "kT0", "kT1", "Vaug0", "Vaug1", "PT0", "PT1", "PT2", "Kc0", "Kc1", "Kc2", "Kc3", "Vc0", "Vc1", "Vc2", "Vc3",
              "KTs0", "KTs1", "KTs2", "KTs3", "PTn", "PTc0", "PTc1", "PTc2", "PTc3", "vstg0", "vstg1", "vstg2", "vstg3", "ztmp", "rscr",
              "vn", "lng", "lnb", "va0", "va1", "va2", "vb2", "bsp", "hbsp", "wspT", "wspTs", "wspn", "wspb", "wsps", "wspsb",
              "mgT", "x40", "x41", "o40", "o41", "sq4"]
             + [f"tA{i}{j}" for i in range(2) for j in range(5)] + [f"tM{i}{j}" for i in range(2) for j in range(6)])
RYKEYS = set(["ACC0", "ACC1", "SZB0", "SZB1", "RD", "yaT"] + [f"kstgA{i}" for i in range(6)])

class Prog:
    def __init__(self, nc):
        self.nc = nc
        self.ops = {e: [] for e in ("pe", "act", "dve", "pool", "sp")}
        self.cnt = {e: 0 for e in COMPUTE}
        self.seen = {e: {} for e in self.ops}
        self.last_w = {}
        self.readers = {}
        self.dma_cnt = {}
        self.final = {}
        self.alias = {}

    def _expand(self, reads, writes):
        reads = list(reads)
        writes = list(writes)
        for big in ("hT", "vn"):
            if big in reads:
                reads = [k for k in reads if k != big] + [f"{big}{i}" for i in range(17)]
        extra = []
        for k in reads + writes:
            a = "RX_ep" if k in RXKEYS else ("RY_ep" if k in RYKEYS else None)
            if a is not None and a not in extra:
                extra.append(a)
        writes = writes + [k for k in reads if k.startswith("ps") and k not in writes]
        return reads + extra, writes

    def _deps(self, eng, reads, writes):
        need = {}

        def add(tok, kind, key=None):
            sk, val, peng = tok
            if peng == eng and eng == "pe" and key != "pe_ser":
                return
            if need.get(sk, 0) < val:
                need[sk] = val

        for k in reads:
            w = self.last_w.get(k)
            if w is not None:
                add(w, "raw", k)
        for k in writes:
            w = self.last_w.get(k)
            if w is not None:
                add(w, "waw", k)
            for (sk, peng), val in self.readers.get(k, {}).items():
                add((sk, val, peng), "war")
        out = []
        for sk, val in need.items():
            if self.seen[eng].get(sk, 0) >= val:
                continue
            self.seen[eng][sk] = val
            out.append((sk, val))
        return out

    def _commit(self, tok, reads, writes):
        for k in writes:
            self.last_w[k] = tok
            self.readers[k] = {}
        sk, val, peng = tok
        for k in reads:
            d = self.readers.setdefault(k, {})
            if d.get((sk, peng), 0) < val:
                d[(sk, peng)] = val

    def op(self, eng, fn, reads=(), writes=()):
        reads, writes = self._expand(reads, writes)
        waits = self._deps(eng, reads, writes)
        self.cnt[eng] += 1
        tok = (eng, self.cnt[eng], eng)
        self.ops[eng].append((fn, waits, ("eng", eng)))
        self._commit(tok, reads, writes)
        return tok

    def dma(self, eng, sem, fn, reads=(), writes=(), final=False):
        reads, writes = self._expand(reads, writes)
        waits = self._deps(eng, reads, writes)
        self.dma_cnt[sem] = self.dma_cnt.get(sem, 0) + 16
        tok = ("dma:" + sem, self.dma_cnt[sem], "dma")
        self.ops[eng].append((fn, waits, ("dma", sem)))
        self._commit(tok, reads, writes)
        if final:
            self.final["dma:" + sem] = self.dma_cnt[sem]
        return tok

    def dma_group(self, eng, sem, fns, reads=(), writes=(), final=False):
        reads, writes = self._expand(reads, writes)
        waits = self._deps(eng, reads, writes)
        for j, fn in enumerate(fns):
            self.dma_cnt[sem] = self.dma_cnt.get(sem, 0) + 16
            self.ops[eng].append((fn, waits if j == 0 else [], ("dma", sem)))
        tok = ("dma:" + sem, self.dma_cnt[sem], "dma")
        self._commit(tok, reads, writes)
        if final:
            self.final["dma:" + sem] = self.dma_cnt[sem]
        return tok

    def emit(self):
        nc = self.nc
        targets = {e: set() for e in COMPUTE}
        for engname, lst in self.ops.items():
            for fn, waits, kind in lst:
                for sk, val in waits:
                    if sk in targets:
                        targets[sk].add(val)
        for e in COMPUTE:
            if self.cnt[e]:
                targets[e].add(self.cnt[e])
        rank = {}
        for e in COMPUTE:
            for r, idx in enumerate(sorted(targets[e])):
                rank[(e, idx)] = r + 1
        with contextlib.ExitStack() as st:
            sems = {}
            for e in COMPUTE:
                sems[e] = st.enter_context(nc.semaphore("s_" + e))
            for name in self.dma_cnt:
                sems["dma:" + name] = st.enter_context(nc.semaphore("d_" + name))
            block = st.enter_context(nc.Block())
            prog = self

            def run(engname, eng):
                idx = 0
                for fn, waits, kind in prog.ops[engname]:
                    for sk, val in waits:
                        if sk in targets:
                            eng.wait_ge(sems[sk], rank[(sk, val)])
                        else:
                            eng.wait_ge(sems[sk], val)
                    ins = fn(eng)
                    if kind[0] == "eng":
                        idx += 1
                        if (kind[1], idx) in rank:
                            ins.then_inc(sems[kind[1]], 1)
                    else:
                        ins.then_inc(sems["dma:" + kind[1]], 16)

            @block.tensor
            def _(eng):
                run("pe", eng)

            @block.scalar
            def _(eng):
                run("act", eng)

            @block.vector
            def _(eng):
                run("dve", eng)

            @block.gpsimd
            def _(eng):
                run("pool", eng)

            @block.sync
            def _(eng):
                run("sp", eng)
                for sk, val in prog.final.items():
                    eng.wait_ge(sems[sk], val)
                for e in COMPUTE:
                    if prog.cnt[e]:
                        eng.wait_ge(sems[e], rank[(e, prog.cnt[e])])


class Region:
    def __init__(self, tensor, nelem, name):
        self.t = tensor
        self.n = nelem
        self.name = name
        self.off = 0

    def reset(self):
        self.off = 0

    def alloc(self, free_shape, dt):
        n = int(np.prod(free_shape))
        nb = n * (2 if dt == F32 else 1)
        nb = (nb + 1) // 2 * 2
        assert self.off + nb <= self.n, (self.name, self.off, nb, self.n)
        v = self.t[:, self.off:self.off + nb]
        self.off += nb
        if dt == F32:
            v = v.bitcast(F32)
        if len(free_shape) == 2:
            v = v.rearrange("p (a b) -> p a b", b=free_shape[1])
        elif len(free_shape) == 3:
            v = v.rearrange("p (a b c) -> p a b c", b=free_shape[1], c=free_shape[2])
        return v


def build_nc(debug=False):
    nc = bass.Bass("TRN2", target_bir_lowering=False)
    din = lambda n, s: nc.dram_tensor(n, s, F32, kind="ExternalInput").ap()
    dout = lambda n, s: nc.dram_tensor(n, s, F32, kind="ExternalOutput").ap()
    xp = din("xp", [S, D])
    xs = din("xs", [NS, D])
    cc = din("cc", [5, D])
    ck = [din("ck0", [4, 128, 2, 512]), din("ck1", [4, 512, 2, 512]), din("ck2", [4, 2048, 2, 512])]
    w_cond = din("w_cond", [D, 3 * D])
    b_cond = din("b_cond", [3 * D])
    g_pre = din("g_pre", [D])
    w_in = din("w_in", [D, 10240])
    ln_g = din("ln_v_g", [D])
    ln_b = din("ln_v_b", [D])
    w_sp = din("w_spatial", [8, 128, 128])
    b_sp = din("b_spatial", [8, 128])
    w_pa = din("w_proj_a", [D, D])
    w_pb = din("w_proj_b", [512, D])
    w_out = din("w_out", [D, D])
    g_post = din("g_post", [D])
    yp = dout("yp", [S, D])
    ys = dout("ys", [NS, D])
    kvp = [dout("kvp0", [128, 2, 512]), dout("kvp1", [512, 2, 512]), dout("kvp2", [2048, 2, 512])]
    kvs = [dout("kvs0", [NS, 2, 512]), dout("kvs1", [NS, 2, 512]), dout("kvs2", [NS, 2, 512])]
    vch = dout("vch", [NS, D])

    st = contextlib.ExitStack()
    with st:
        sb = lambda n, s, d: st.enter_context(nc.sbuf_tensor(n, s, d))
        hT = sb("hT", [128, 8, T], BF16)
        ybT = sb("ybT", [128, 4, T], BF16)
        GGp = sb("GGp", [128, D], F32)
        GGs = sb("GGs", [128, D], F32)
        slabs = [sb(f"slab{i}", [128, 8, 512], BF16) for i in range(4)]
        RXN = 41 * 1024
        RYN = 8 * T
        RXt = sb("RX", [128, RXN], BF16)
        RYt = sb("RY", [128, RYN], BF16)
        ident = sb("ident", [128, 128], BF16)
        MKN = sb("MKN", [128, 3, 128], BF16)
        MC = sb("MC", [128, 8], BF16)
        MN = sb("MN", [32, 3, 32], BF16)
        tril = sb("tril", [128, 128], F32)
        ss = sb("ss", [128, 128], F32)
        st2 = sb("st2", [128, 128], F32)
        nhalf = sb("nhalf", [128, 1], F32)
        bnst = [sb(f"bnst{i}", [128, 12], F32) for i in range(3)]
        ps = [st.enter_context(nc.psum_tensor(f"ps{i}", [128, 512], F32)) for i in range(8)]
        RX = Region(RXt, RXN, "RX")
        RY = Region(RYt, RYN, "RY")

        P = Prog(nc)
        psk = [f"ps{i}" for i in range(8)]

        def psb(i):
            return ps[i][:, :].bitcast(BF16)

        def MM(out, lhsT, rhs, start, stop, reads, writes):
            P.op("pe", lambda e: e.matmul(out, lhsT=lhsT, rhs=rhs, start=start, stop=stop, skip_group_check=True),
                 reads, writes)

        def TR(out, in_, idn, reads, writes):
            P.op("pe", lambda e: e.transpose(out=out, in_=in_, identity=idn), reads, writes)

        def ACT(out, in_, func, reads, writes, scale=1.0, bias=None, accum=None):
            def f(e):
                kw = {}
                if bias is not None:
                    kw["bias"] = bias
                if accum is not None:
                    kw["accum_out"] = accum
                return e.activation(out=out, in_=in_, func=func, scale=scale, **kw)
            P.op("act", f, reads, writes)

        def TT(eng, out, in0, in1, op, reads, writes):
            P.op(eng, lambda e: e.tensor_tensor(out=out, in0=in0, in1=in1, op=op), reads, writes)

        def TS(eng, out, in0, s1, s2, op0, op1, reads, writes):
            if s2 is None:
                P.op(eng, lambda e: e.tensor_single_scalar(out=out, in_=in0, scalar=s1, op=op0), reads, writes)
            else:
                P.op(eng, lambda e: e.tensor_scalar(out=out, in0=in0, scalar1=s1, scalar2=s2, op0=op0, op1=op1),
                     reads, writes)

        def STT(eng, out, in0, scalar, in1, op0, op1, reads, writes):
            P.op(eng, lambda e: e.scalar_tensor_tensor(out=out, in0=in0, scalar=scalar, in1=in1, op0=op0, op1=op1),
                 reads, writes)

        def CP(eng, out, in_, reads, writes):
            if eng == "act":
                P.op("act", lambda e: e.activation(out=out, in_=in_, func=AF.Copy), reads, writes)
            else:
                P.op(eng, lambda e: e.tensor_copy(out=out, in_=in_), reads, writes)

        def MS(eng, ap, val, writes):
            P.op(eng, lambda e: e.memset(ap, val), (), writes)

        def ASEL(out, in_, pattern, cmp, fill, base, cm, reads, writes):
            P.op("pool", lambda e: e.affine_select(out=out, in_=in_, pattern=pattern, compare_op=cmp, fill=fill,
                                                   base=base, channel_multiplier=cm), reads, writes)

        dma_id = [0]

        def LD(out, in_, writes, sem=None, reads=(), q="sp", final=False):
            if sem is None:
                dma_id[0] += 1
                sem = f"m{dma_id[0]}"
            P.dma(q, sem, lambda e: e.dma_start(out=out, in_=in_), reads, writes, final=final)

        slab_ctr = [0]

        def take_slabs(n):
            ids = [(slab_ctr[0] + i) % 4 for i in range(n)]
            slab_ctr[0] += n
            return ids

        def load_slab(si, pieces):
            key = f"slab{si}"
            fns = []
            for (off, src) in pieces:
                nk = src.shape[0] // 128
                ncol = src.shape[1]
                fns.append(lambda e, off=off, src=src, nk=nk, ncol=ncol: e.dma_start(
                    out=slabs[si][:, 0:nk, off:off + ncol], in_=src.rearrange("(k p) c -> p k c", p=128)))
            if fns:
                P.dma_group("pool", key, fns, (), [key])

        units = []

        def unit(pieces_per_slab, fn):
            units.append((pieces_per_slab, fn))

        def run_units():
            pend = {}

            def issue(i):
                pieces_per_slab, fn = units[i]
                ids = take_slabs(len(pieces_per_slab))
                for si, pcs in zip(ids, pieces_per_slab):
                    load_slab(si, pcs)
                pend[i] = ids
            for j in range(min(2, len(units))):
                issue(j)
            for i in range(len(units)):
                if i + 2 < len(units):
                    issue(i + 2)
                units[i][1](pend[i])

        MS("pool", nhalf[:], -0.5, ["nhalf"])
        MS("pool", ident[:], 1.0, ["ident"])
        ASEL(ident[:], ident[:], [[-1, 128]], ALU.is_equal, 0.0, 0, 1, ["ident"], ["ident"])
        RX.reset()
        bc_b = RX.alloc([3 * D], F32)
        bc_gpre = RX.alloc([D], F32)
        bc_gpost = RX.alloc([D], F32)
        MA = RX.alloc([2, D], F32)
        MAs = RX.alloc([2, D], F32)
        xt = [RX.alloc([D], F32) for _ in range(3)]
        t1 = [RX.alloc([D], F32) for _ in range(2)]
        hb = [RX.alloc([D], BF16) for _ in range(2)]
        junk = [RX.alloc([D], BF16) for _ in range(2)]
        c_sb = RX.alloc([D], F32)
        c_th = RX.alloc([D], F32)
        c_bf = RX.alloc([D], BF16)
        cTp = RX.alloc([8, 128], BF16)
        cTs = RX.alloc([8, 32], BF16)
        kstgA = [RY.alloc([512], F32) for _ in range(6)]

        LD(c_sb[0:5, :], cc[:, :], ["c_sb"])
        LD(bc_b[:, :], b_cond.partition_broadcast(128), ["bc_b"])
        LD(bc_gpre[:, :], g_pre.partition_broadcast(128), ["bc_gpre"])
        LD(bc_gpost[:, :], g_post.partition_broadcast(128), ["bc_gpost"])
        ACT(c_th[0:5, :], c_sb[0:5, :], AF.Tanh, ["c_sb"], ["c_th"], scale=0.5)
        STT("dve", c_th[0:5, :], c_th[0:5, :], 1.0, c_sb[0:5, :], ALU.add, ALU.mult, ["c_th", "c_sb"], ["c_th"])
        TS("dve", c_bf[0:5, :], c_th[0:5, :], 0.5, None, ALU.mult, None, ["c_th"], ["c_bf"])
        pT7 = psb(7).rearrange("p (k c) -> p k c", c=128)
        for k in range(8):
            TR(pT7[:, k, 0:5], c_bf[0:5, k * 128:(k + 1) * 128], ident[0:5, 0:5], ["c_bf", "ident"], [psk[7]])
        CP("dve", cTp[:, :, :], pT7[:, :, 0:1].broadcast_to([128, 8, 128]), [psk[7]], ["cTp"])
        for b in range(4):
            CP("dve", cTs[:, :, 8 * b:8 * b + 8], pT7[:, :, 1 + b:2 + b].broadcast_to([128, 8, 8]), [psk[7]], ["cTs"])

        def mod_unit(j):
            def fn(ids):
                sl = slabs[ids[0]]
                sk = f"slab{ids[0]}"
                for k in range(8):
                    MM(ps[0][:, :], cTp[:, k, :], sl[:, k, :], k == 0, k == 7, ["cTp", sk], [psk[0]])
                for k in range(8):
                    MM(ps[1][0:32, :], cTs[:, k, :], sl[:, k, :], k == 0, k == 7, ["cTs", sk], [psk[1]])
                cols = slice(512 * j, 512 * j + 512)
                if j < 4:
                    dstp = MA.rearrange("p a b -> p (a b)")[:, cols]
                    dsts = MAs.rearrange("p a b -> p (a b)")[0:32, cols]
                    kp, ks_ = "MA", "MAs"
                else:
                    c2 = slice(512 * (j - 4), 512 * (j - 4) + 512)
                    dstp = GGp[:, c2]
                    dsts = GGs[0:32, c2]
                    kp, ks_ = "GGp", "GGs"
                TT("dve", dstp, ps[0][:, :], bc_b[:, cols], ALU.add, [psk[0], "bc_b"], [kp])
                TT("dve", dsts, ps[1][0:32, :], bc_b[0:32, cols], ALU.add, [psk[1], "bc_b"], [ks_])
            return fn

        for j in range(6):
            unit([[(0, w_cond[:, 512 * j:512 * j + 512])]], mod_unit(j))

        def mod_finish(ids):
            STT("dve", MA[:, 1, :], MA[:, 1, :], 1.0, bc_gpre[:, :], ALU.add, ALU.mult, ["MA", "bc_gpre"], ["MA"])
            STT("dve", MAs[0:32, 1, :], MAs[0:32, 1, :], 1.0, bc_gpre[0:32, :], ALU.add, ALU.mult, ["MAs", "bc_gpre"], ["MAs"])
            TT("pool", GGp[:, :], GGp[:, :], bc_gpost[:, :], ALU.mult, ["GGp", "bc_gpost"], ["GGp"])
            TT("pool", GGs[0:32, :], GGs[0:32, :], bc_gpost[0:32, :], ALU.mult, ["GGs", "bc_gpost"], ["GGs"])
            def s1_A1(i):
                n = 128 if i < 16 else 32
                xb_ = xt[i % 3]
                xk = f"xt{i % 3}"
                src = xp[i * 128:(i + 1) * 128, :] if i < 16 else xs[:, :]
                LD(xb_[0:n, :], src, [xk], sem=f"x{i % 3}")
                jk = f"junk{i % 2}"
                sk1 = f"s1_{i}"
                ACT(junk[i % 2][0:n, :], xb_[0:n, :], AF.Square, [xk], [jk, sk1], accum=ss[0:n, i:i + 1])
                TS("pool", st2[0:n, i:i + 1], ss[0:n, i:i + 1], 1.0 / D, EPS, ALU.mult, ALU.add, [sk1], [sk1])
                TT("pool", st2[0:n, i:i + 1], st2[0:n, i:i + 1], nhalf[0:n, 0:1], ALU.pow, [sk1, "nhalf"], [sk1])

            def s1_A2(i):
                n = 128 if i < 16 else 32
                xb_ = xt[i % 3]
                xk = f"xt{i % 3}"
                sk1 = f"s1_{i}"
                A_ = MA if i < 16 else MAs
                ak = "MA" if i < 16 else "MAs"
                tk1 = f"t1{i % 2}"
                STT("dve", t1[i % 2][0:n, :], xb_[0:n, :], st2[0:n, i:i + 1], A_[0:n, 1, :], ALU.mult, ALU.mult,
                    [xk, sk1, ak], [tk1])
                TT("dve", hb[i % 2][0:n, :], t1[i % 2][0:n, :], A_[0:n, 0, :], ALU.add, [tk1, ak], [f"hb{i % 2}"])

            def s1_B(i):
                n = 128 if i < 16 else 32
                hbi = hb[i % 2]
                hk = f"hb{i % 2}"
                pb = 6 + (i % 2)
                pTv = psb(pb).rearrange("p (k c) -> p k c", c=128)
                for k in range(8):
                    TR(pTv[:, k, 0:n], hbi[0:n, k * 128:(k + 1) * 128], ident[0:n, 0:n], [hk, "ident"], [psk[pb]])
                CP("act", hT[:, :, i * 128:i * 128 + n], pTv[:, :, 0:n], [psk[pb]], [f"hT{i}"])

            def s1_K(i):
                sl = slabs[ids[0]]
                sk = f"slab{ids[0]}"
                n = 128 if i < 16 else 32
                pb = i % 2
                for k in range(8):
                    MM(ps[pb][0:n, :], hT[:, k, i * 128:i * 128 + n], sl[:, k, :], k == 0, k == 7, [f"hT{i}", sk], [psk[pb]])
                stg = kstgA[i % 6]
                skey = f"kstgA{i % 6}"
                CP("act", stg[0:n, :], ps[pb][0:n, :], [psk[pb]], [skey])
                if i < 16:
                    LD(kvp[2][i * 128:i * 128 + 128, 0, :], stg[:, :], [], sem=skey, reads=[skey], final=True, q="act")
                else:
                    LD(kvs[2][:, 0, :], stg[0:32, :], [], sem=skey, reads=[skey], final=True, q="act")

            mgen = make_masks()
            for i in range(23):
                for _ in range(2):
                    next(mgen, None)
                if i < 17:
                    s1_A1(i)
                if 6 <= i < 23:
                    s1_K(i - 6)
                if 1 <= i < 18:
                    s1_A2(i - 1)
                if 2 <= i < 19:
                    s1_B(i - 2)
            for _ in mgen:
                pass

        unit([[(0, w_in[:, 4608 + 1024:4608 + 1536])]], mod_finish)

        def kout_unit(g):
            win, d = PAIRS[g]
            tiles = list(range(16 - win // 128, 16)) + [16]

            def fn(ids):
                sl = slabs[ids[0]]
                sk = f"slab{ids[0]}"
                for ti, i in enumerate(tiles):
                    n = 128 if i < 16 else 32
                    pb = (0, 1, 6, 7)[ti % 4]
                    for k in range(8):
                        MM(ps[pb][0:n, :], hT[:, k, i * 128:i * 128 + n], sl[:, k, :], k == 0, k == 7, ["hT", sk], [psk[pb]])
                    stg = kstg[ti % 4]
                    skey = f"kstg{ti % 4}"
                    CP("act" if ti % 2 == 0 else "dve", stg[0:n, :], ps[pb][0:n, :], [psk[pb]], [skey])
                    if i < 16:
                        r0 = i * 128 - (S - win)
                        LD(kvp[g][r0:r0 + 128, 0, :], stg[:, :], [], sem=skey, reads=[skey], final=True)
                    else:
                        LD(kvs[g][:, 0, :], stg[0:32, :], [], sem=skey, reads=[skey], final=True)
            return fn

        def barrier(regions):
            for rname in regions:
                c_ = 127 if rname == "RX" else 126
                MS("pool", ss[:, c_:c_ + 1], 0.0, [rname + "_ep"])

        RX.reset()
        kstg = [RX.alloc([512], F32) for _ in range(4)]
        for nm in ("kstg0", "kstg1"):
            P.alias[nm] = "RX_ep"
        qT = [RX.alloc([T], BF16) for _ in range(2)]
        kT = [RX.alloc([T], BF16) for _ in range(2)]
        Vaug = [RX.alloc([17, 2, 128], BF16) for _ in range(2)]
        PT = [RX.alloc([256], BF16) for _ in range(3)]
        Kc = [RX.alloc([8, 128], BF16) for _ in range(4)]
        Vc = [RX.alloc([8, 2, 128], BF16) for _ in range(4)]
        KTs = [RX.alloc([8, 128], BF16) for _ in range(4)]
        PTn = RX.alloc([64], BF16)
        PTc = [RX.alloc([16], BF16) for _ in range(4)]
        vstg = [RX.alloc([128], F32) for _ in range(4)]
        ztmp = RX.alloc([512], F32)
        RY.reset()
        ACC = [RY.alloc([T], F32) for _ in range(2)]
        SZB = [RY.alloc([T], BF16) for _ in range(2)]
        RD = RY.alloc([T], F32)
        s3keys = (["qT0", "qT1", "kT0", "kT1", "Vaug0", "Vaug1", "PT0", "PT1", "PT2", "Kc0", "Kc1", "Vc0", "Vc1",
                   "KTs", "PTn", "PTc", "vstg0", "vstg1", "vstg2", "vstg3", "ztmp"])
        for nm in s3keys:
            P.alias[nm] = "RX_ep"
        for nm in ("ACC0", "ACC1", "SZB", "RD"):
            P.alias[nm] = "RY_ep"
        for nm in ("bc_b", "bc_gpre", "bc_gpost", "MA", "MAs", "xt0", "xt1", "xt2", "t10", "t11", "hb0", "hb1", "junk0", "junk1", "c_sb", "c_th",
                   "c_bf", "cTp", "cTs"):
            P.alias[nm] = "RX_ep"

        def make_masks():
            MS("pool", MKN[:], 0.0, ["MKN"])
            yield
            for idx in (0, 2):
                ASEL(MKN[:, idx, :], MKN[:, idx, :], [[-1, 128]], ALU.is_ge, -30000.0, 0, 1, ["MKN"], ["MKN"])
                yield
            ASEL(MKN[:, 1, :], MKN[:, 1, :], [[1, 128]], ALU.is_ge, -30000.0, 0, -1, ["MKN"], ["MKN"])
            yield
            MS("pool", MC[:], 1.0, ["MC"])
            yield
            ASEL(MC[:], MC[:], [[-1, 8]], ALU.is_ge, 0.0, 0, 1, ["MC"], ["MC"])
            yield
            MS("pool", MN[:], 1.0, ["MN"])
            yield
            ASEL(MN[:, 0, :], MN[:, 0, :], [[1, 32]], ALU.is_ge, 0.0, 0, -1, ["MN"], ["MN"])
            yield
            ASEL(MN[:, 2, :], MN[:, 2, :], [[1, 32]], ALU.is_equal, 0.0, 0, -1, ["MN"], ["MN"])
            yield
            ASEL(MN[:, 1, :], MN[:, 1, :], [[1, 32]], ALU.is_equal, 0.0, -4, -1, ["MN"], ["MN"])
            yield
            for g in range(2):
                for b in range(4):
                    blk = MN[:, g, 8 * b:8 * b + 8]
                    ASEL(blk, blk, [[0, 8]], ALU.is_ge, 0.0, -8 * b, 1, ["MN"], ["MN"])
                    yield
                    ASEL(blk, blk, [[0, 8]], ALU.is_ge, 0.0, 8 * b + 7, -1, ["MN"], ["MN"])
                    yield
            TT("pool", MN[:, 1, :], MN[:, 1, :], MN[:, 2, :], ALU.add, ["MN"], ["MN"])
            yield
            MS("pool", tril[:], 1.0, ["tril"])
            yield
            ASEL(tril[:], tril[:], [[-1, 128]], ALU.is_ge, 0.0, 0, 1, ["tril"], ["tril"])
            yield


        def s3_begin(ids):
            barrier(["RX", "RY"])
            for b in range(2):
                MS("dve", Vaug[b][:, :, :, :], 1.0, [f"Vaug{b}"])
            for b in range(4):
                MS("dve", Vc[b][:, :, :, :], 1.0, [f"Vc{b}"])

        unit([], s3_begin)
        for g in range(2):
            unit([[(0, w_in[:, 4608 + 512 * g:4608 + 512 * g + 512])]], kout_unit(g))

        st_ring = [2, 3, 7]
        pj_ctr = [0]

        def next_pj():
            pj_ctr[0] += 1
            return pj_ctr[0] % 2

        def w_pieces(u):
            p, g = divmod(u, 3)
            pcs = [(0, w_in[:, 3072 + 512 * g + 128 * p:3072 + 512 * g + 128 * p + 128]),
                   (128, w_in[:, 4608 + 512 * g + 128 * p:4608 + 512 * g + 128 * p + 128]),
                   (256, w_in[:, 6144 + 512 * g + 128 * p:6144 + 512 * g + 128 * p + 128])]
            if g == 0:
                pcs.append((384, w_in[:, 7680 + 128 * p:7680 + 128 * p + 128]))
            return pcs

        def mk_tok(d):
            def tok(r, n, cnt):
                s0 = d * 128 * n + r
                return slice(s0, s0 + d * (cnt - 1) + 1, d) if d > 1 else slice(s0, s0 + cnt)
            return tok

        def proj_gen(u, sl, sk):
            p, g = divmod(u, 3)
            win, d = PAIRS[g]
            nb = 16 // d
            tok = mk_tok(d)
            ub = u % 2
            qk_, kk_, vk_ = f"qT{ub}", f"kT{ub}", f"Vaug{ub}"
            q_, k_, V_ = qT[ub], kT[ub], Vaug[ub]
            szb, szk = SZB[p % 2], f"SZB{p % 2}"

            def fm(coff, consume):
                for (c0, cn) in CH:
                    pb = next_pj()
                    for k in range(8):
                        MM(ps[pb][:, 0:cn], sl[:, k, coff:coff + 128], hT[:, k, c0:c0 + cn], k == 0, k == 7,
                           [sk, "hT"], [psk[pb]])
                    consume(pb, c0, cn)
                    yield

            if g == 0:
                def cons_z(pb, c0, cn):
                    ACT(ztmp[:, 0:cn], ps[pb][:, 0:cn], AF.Tanh, [psk[pb]], ["ztmp"], scale=0.5)
                    STT("dve", szb[:, c0:c0 + cn], ztmp[:, 0:cn], 1.0, ps[pb][:, 0:cn], ALU.add, ALU.mult,
                        ["ztmp", psk[pb]], [szk])
                yield from fm(384, cons_z)

            def perm_store(dst, key, pb, c0, cn):
                if d > 1 and cn == 512:
                    o = dst[:, 0:S].rearrange("p (r m) -> p r m", r=d)[:, :, c0 // d:(c0 + cn) // d]
                    i_ = ps[pb][:, 0:cn].rearrange("p (m r) -> p r m", r=d)
                    CP("act", o, i_, [psk[pb]], [key])
                else:
                    CP("act", dst[:, c0:c0 + cn], ps[pb][:, 0:cn], [psk[pb]], [key])

            def cons_q(pb, c0, cn):
                perm_store(q_, qk_, pb, c0, cn)

            def cons_k(pb, c0, cn):
                perm_store(k_, kk_, pb, c0, cn)
            yield from fm(0, cons_q)
            yield from fm(128, cons_k)
            blocks = [(r, n) for r in range(d) for n in range(nb)]
            for b0 in range(0, 16, 4):
                pb = next_pj()
                pv = ps[pb][:, :].rearrange("p (s c) -> p s c", c=128)
                for s_ in range(4):
                    r, n = blocks[b0 + s_]
                    for k in range(8):
                        MM(pv[:, s_, :], hT[:, k, tok(r, n, 128)], sl[:, k, 256:384], k == 0, k == 7,
                           ["hT", sk], [psk[pb]])
                CP("dve", V_[:, b0:b0 + 4, 0, 0:64], pv[:, :, 0:64], [psk[pb]], [vk_])
                CP("dve", V_[:, b0:b0 + 4, 1, 64:128], pv[:, :, 64:128], [psk[pb]], [vk_])
                for s_ in range(4):
                    r, n = blocks[b0 + s_]
                    if n == nb - 1:
                        vi = (b0 + s_) % 4
                        CP("dve", vstg[vi][:, :], pv[:, s_, :], [psk[pb]], [f"vstg{vi}"])
                        r0 = d * 128 * n + r - (S - win)
                        dst = kvp[g][r0:r0 + d * 127 + 1:d, 1, p * 128:(p + 1) * 128] if d > 1 else \
                            kvp[g][r0:r0 + 128, 1, p * 128:(p + 1) * 128]
                        LD(dst, vstg[vi][:, :], [], sem=f"vstg{vi}", reads=[f"vstg{vi}"], final=True)
                yield
            pb = next_pj()
            for k in range(8):
                MM(ps[pb][0:32, 0:128], hT[:, k, S:T], sl[:, k, 256:384], k == 0, k == 7, ["hT", sk], [psk[pb]])
            CP("dve", V_[0:32, 16, 0, 0:64], ps[pb][0:32, 0:64], [psk[pb]], [vk_])
            CP("dve", V_[0:32, 16, 1, 64:128], ps[pb][0:32, 64:128], [psk[pb]], [vk_])
            CP("dve", vstg[0][0:32, :], ps[pb][0:32, 0:128], [psk[pb]], ["vstg0"])
            LD(kvs[g][:, 1, p * 128:(p + 1) * 128], vstg[0][0:32, :], [], sem="vstg0", reads=["vstg0"], final=True)
            yield

        def attend_gen(u):
            p, g = divmod(u, 3)
            win, d = PAIRS[g]
            nb = 16 // d
            tok = mk_tok(d)
            ub = u % 2
            qk_, kk_, vk_ = f"qT{ub}", f"kT{ub}", f"Vaug{ub}"
            q_, k_, V_ = qT[ub], kT[ub], Vaug[ub]
            szb, szk = SZB[p % 2], f"SZB{p % 2}"
            nres = (1, 4, 8)[g]
            nq = (8, 2, 1)[g]
            osb = 6
            for b in range(4):
                kck, vck = f"Kc{b}", f"Vc{b}"
                csrc = ck[g][b].rearrange("(m r) t f -> m r t f", r=d)
                P.dma("pool", kck, lambda e, b=b, csrc=csrc: e.dma_start(
                    out=Kc[b][:, 0:nres, :], in_=csrc[:, 0:nres, 0, p * 128:(p + 1) * 128]), (), [kck])
                P.dma_group("pool", vck, [lambda e, b=b, csrc=csrc, e2=e2: e.dma_start(
                    out=Vc[b][:, 0:nres, e2, 64 * e2:64 * e2 + 64],
                    in_=csrc[:, 0:nres, 1, p * 128 + 64 * e2:p * 128 + 64 * e2 + 64]) for e2 in range(2)], (), [vck])
            if g < 2:
                rounds = [[(r, n0 + s_) for s_ in range(4)] for r in range(d) for n0 in range(0, nb, 4)]
            else:
                rounds = [[(r0 + s_, 0) for s_ in range(4)] for r0 in range(0, 16, 4)]
            items = []
            for rnd in rounds:
                for e in range(2):
                    sub = []
                    if g < 2:
                        r, n0 = rnd[0]
                        if n0 >= 1:
                            sub.append(((r, n0 - 1), 0, 1, MKN[:, 0, :]))
                        for s_ in range(4):
                            if s_ < 3:
                                sub.append(((r, n0 + s_), s_, 2, MKN[:, 1:3, :].rearrange("p a b -> p (a b)")))
                            else:
                                sub.append(((r, n0 + s_), s_, 1, MKN[:, 1, :]))
                    else:
                        for s_ in range(4):
                            sub.append((rnd[s_], s_, 1, MKN[:, 1, :]))
                    for ii, (kb, s0, nsl, msk) in enumerate(sub):
                        items.append(dict(rnd=rnd, e=e, kb=kb, s0=s0, nsl=nsl, msk=msk, first=(ii == 0),
                                          last=(ii == len(sub) - 1)))

            def emit_st(it, i):
                e = it["e"]
                hs = slice(64 * e, 64 * e + 64)
                N = 128 * it["nsl"]
                sbk = st_ring[i % 3]
                pti = i % 3
                kr, kn = it["kb"]
                qr, qn = it["rnd"][it["s0"]]
                LL = S // d
                MM(ps[sbk][:, 0:N], k_[hs, kr * LL + 128 * kn:kr * LL + 128 * kn + 128],
                   q_[hs, qr * LL + 128 * qn:qr * LL + 128 * qn + N], True, False, [kk_, qk_], [psk[sbk]])
                MM(ps[sbk][:, 0:N], ident[:, :], it["msk"], False, True, ["ident", "MKN"], [psk[sbk]])
                ACT(PT[pti][:, 0:N], ps[sbk][:, 0:N], AF.Exp, [psk[sbk]], [f"PT{pti}"], scale=0.125)

            def emit_pv(it, i):
                e = it["e"]
                ob = 4 + e
                N = 128 * it["nsl"]
                pti = i % 3
                kr, kn = it["kb"]
                s0 = it["s0"]
                rnd = it["rnd"]
                MM(ps[ob][:, 128 * s0:128 * s0 + N], V_[:, kr * nb + kn, e, :], PT[pti][:, 0:N], it["first"], False,
                   [vk_, f"PT{pti}"], [psk[ob]])
                if it["last"]:
                    if g == 0:
                        dst = ACC[e][:, 128 * rnd[0][1]:128 * rnd[0][1] + 512]
                        src = ps[ob][:, :]
                    elif g == 1:
                        r = rnd[0][0]
                        dst = ACC[e][:, r:S:4]
                        src = ps[ob][:, :]
                    else:
                        r0 = rnd[0][0]
                        dst = ACC[e][:, 0:S].rearrange("p (j r) -> p r j", r=16)[:, r0:r0 + 4, :]
                        src = ps[ob][:, :].rearrange("p (s c) -> p s c", c=128)
                    if g == 0:
                        CP("dve", dst, src, [psk[ob]], [f"ACC{e}"])
                    else:
                        TT("dve", dst, src, dst, ALU.add, [psk[ob], f"ACC{e}"], [f"ACC{e}"])

            SK = 2
            for i in range(len(items) + SK):
                if i < len(items):
                    emit_st(items[i], i)
                if i - SK >= 0:
                    emit_pv(items[i - SK], i - SK)
                yield

            def finalize(cs_):
                ACT(RD[0:64, cs_], ACC[0][64:128, cs_], AF.Ln, ["ACC0"], ["RD"])
                ACT(RD[64:128, cs_], ACC[1][0:64, cs_], AF.Ln, ["ACC1"], ["RD"])
                ACT(RD[:, cs_], RD[:, cs_], AF.Exp, ["RD"], ["RD"], scale=-1.0)
                STT("dve", RD[:, cs_], RD[:, cs_], 0.5, szb[:, cs_], ALU.mult, ALU.mult, ["RD", szk], ["RD"])
                TT("dve", ybT[0:64, p, cs_], ACC[0][0:64, cs_], RD[0:64, cs_], ALU.mult, ["ACC0", "RD"], ["ybT"])
                TT("pool", ybT[64:128, p, cs_], ACC[1][64:128, cs_], RD[64:128, cs_], ALU.mult, ["ACC1", "RD"], ["ybT"])

            sbk = st_ring[0]
            for e in range(2):
                hs = slice(64 * e, 64 * e + 64)
                MM(ps[sbk][0:32, 32 * e:32 * e + 32], k_[hs, S:T], q_[hs, S:T], True, True, [kk_, qk_],
                   [psk[sbk], "pe_ser"])
            ACT(PTn[0:32, :], ps[sbk][0:32, 0:64], AF.Exp, [psk[sbk]], ["PTn"], scale=0.125)
            mn = MN[:, g, :]
            mn2 = bass.AP(tensor=mn.tensor, offset=mn.offset, ap=[list(mn.ap[0]), [0, 2], list(mn.ap[1])])
            TT("dve", PTn[0:32, :].rearrange("p (e q) -> p e q", e=2), PTn[0:32, :].rearrange("p (e q) -> p e q", e=2),
               mn2, ALU.mult, ["PTn", "MN"], ["PTn"])
            for e in range(2):
                MM(ps[osb][:, 32 * e:32 * e + 32], V_[0:32, 16, e, :], PTn[0:32, 32 * e:32 * e + 32],
                   (g == 0 and e == 0), False, [vk_, "PTn"], [psk[osb]])
            yield
            for b in range(4):
                pb = next_pj()
                pTv = psb(pb).rearrange("p (k c) -> p k c", c=128)
                for r in range(nres):
                    TR(pTv[:, r, :], Kc[b][:, r, :], ident[:, :], [f"Kc{b}", "ident"], [psk[pb]])
                CP("dve", KTs[b][:, 0:nres, :], pTv[:, 0:nres, :], [psk[pb]], [f"KTs{b}"])
                yield
            for b in range(4):
                sbk = st_ring[(b + 1) % 3]
                stv = ps[sbk][:, 0:16].rearrange("p (r e q) -> p r e q", e=2, q=nq)
                for e in range(2):
                    for r in range(nres):
                        hs = slice(64 * e, 64 * e + 64)
                        q0 = S + 8 * b + (r if g > 0 else 0)
                        qs = slice(q0, q0 + 5, 4) if g == 1 else slice(q0, q0 + nq)
                        ser = ["pe_ser"] if (r == nres - 1 and e == 0) or (r == 0 and e == 1) else []
                        MM(stv[:, r, e, :], KTs[b][hs, r, :], q_[hs, qs], True, True, [f"KTs{b}", qk_], [psk[sbk]] + ser)
                ACT(PTc[b][:, :], ps[sbk][:, 0:16], AF.Exp, [psk[sbk]], [f"PTc{b}"], scale=0.125)
                mc = bass.AP(tensor=MC, offset=0, ap=[[8, 128], [0, nres * 2], [1, nq]])
                ptv = PTc[b][:, :].rearrange("p (a q) -> p a q", q=nq)
                TT("dve", ptv, ptv, mc, ALU.mult, [f"PTc{b}", "MC"], [f"PTc{b}"])
                yield
            for b in range(4):
                ptv4 = PTc[b][:, :].rearrange("p (r e q) -> p r e q", e=2, q=nq)
                for r in range(nres):
                    for e in range(2):
                        o0 = 32 * e + 8 * b + (r if g > 0 else 0)
                        osl = slice(o0, o0 + 5, 4) if g == 1 else slice(o0, o0 + nq)
                        MM(ps[osb][:, osl], Vc[b][:, r, e, :], ptv4[:, r, e, :], False, False, [f"Vc{b}", f"PTc{b}"], [psk[osb]])
                yield
            if g == 2:
                for e in range(2):
                    CP("act", ACC[e][:, S:T], ps[osb][:, 32 * e:32 * e + 32], [psk[osb]], [f"ACC{e}"])
                finalize(slice(0, T))
                yield

        def run_interleaved(a, b, rates):
            done_b = b is None
            acc = 0.0
            for i, _ in enumerate(a):
                if not done_b:
                    acc += rates[i] if i < len(rates) else 1.0
                    while acc >= 1.0 and not done_b:
                        acc -= 1.0
                        try:
                            next(b)
                        except StopIteration:
                            done_b = True
            if not done_b:
                for _ in b:
                    pass

        NU = 12

        def attn_pre(ids):
            for _ in proj_gen(0, slabs[ids[0]], f"slab{ids[0]}"):
                pass

        def attn_unit(u):
            def fn(ids):
                nxt = proj_gen(u + 1, slabs[ids[0]], f"slab{ids[0]}") if u + 1 < NU else None
                g = u % 3
                npr = (40 if g == 0 else 32) + 2
                nsm = 13 + (1 if g == 2 else 0)
                nbg = 20 if (u + 1) % 3 == 0 else 15
                r2 = 0.55
                r1 = max(nbg - r2 * (nsm - 3), 0.0) / npr
                run_interleaved(attend_gen(u), nxt, [r1] * npr + [r2] * nsm)
            return fn

        unit([w_pieces(0)], attn_pre)
        for u in range(NU):
            unit([w_pieces(u + 1)] if u + 1 < NU else [], attn_unit(u))

        RX.reset()
        vn = RX.alloc([17, D], BF16)
        lng = RX.alloc([D], F32)
        lnb = RX.alloc([D], F32)
        va = [RX.alloc([D], F32) for _ in range(3)]
        va2 = RX.alloc([D], BF16)
        bsp = RX.alloc([8, 128], F32)
        wspT = RX.alloc([8, 128], BF16)
        wspTs = RX.alloc([8, 32], BF16)
        wspn = RX.alloc([8, 128], F32)
        wspb = RX.alloc([8, 128], BF16)
        wsps = RX.alloc([8, 32], F32)
        wspsb = RX.alloc([8, 32], BF16)
        tmpA = [[RX.alloc([512], F32) for _ in range(3)] for _ in range(2)]
        hbsp = bsp
        RY.reset()
        yaT = RY.alloc([8, T], BF16)
        for nm in (["vn", "lng", "lnb", "va0", "va1", "va2", "vb2", "bsp", "hbsp", "wspT", "wspTs", "wspn", "wspb", "wsps", "wspsb"]
                   + [f"tA{i}{j}" for i in range(2) for j in range(5)]):
            P.alias[nm] = "RX_ep"
        P.alias["yaT"] = "RY_ep"

        def s2_begin(ids):
            barrier(["RX", "RY"])
            LD(lng[:, :], ln_g.partition_broadcast(128), ["lng"])
            LD(lnb[:, :], ln_b.partition_broadcast(128), ["lnb"])
            LD(bsp[:, :, :].rearrange("p g t -> p (g t)"), b_sp.rearrange("g t -> (g t)").partition_broadcast(128), ["bsp"])
            LD(wspn[:, :, :], w_sp.rearrange("g t s -> t g s"), ["wspn"])
            MS("pool", wsps[0:32, :, :], 0.0, ["wsps"])
            for b in range(4):
                LD(wsps[8 * b:8 * b + 8, :, 8 * b:8 * b + 8], w_sp[:, 0:8, 0:8].rearrange("g t s -> t g s"), ["wsps"])

        def wsp_prep():
            TS("dve", bsp[:, :, :], bsp[:, :, :], 0.5, None, ALU.mult, None, ["bsp"], ["bsp"])
            tri3 = bass.AP(tensor=tril.tensor if hasattr(tril, "tensor") else tril, offset=0, ap=[[128, 128], [0, 8], [1, 128]])
            TT("dve", wspb[:, :, :], wspn[:, :, :], tri3, ALU.mult, ["wspn", "tril"], ["wspb"])
            pTv = psb(6).rearrange("p (k c) -> p k c", c=128)
            for g8 in range(8):
                TR(pTv[:, g8, :], wspb[:, g8, :], ident[:, :], ["wspb", "ident"], [psk[6]])
            CP("dve", wspT[:, :, :], pTv[:, :, :], [psk[6]], ["wspT"])
            tri3s = bass.AP(tensor=tril.tensor if hasattr(tril, "tensor") else tril, offset=0, ap=[[128, 32], [0, 8], [1, 32]])
            TT("dve", wspsb[0:32, :, :], wsps[0:32, :, :], tri3s, ALU.mult, ["wsps", "tril"], ["wspsb"])
            pTs = psb(7).rearrange("p (k c) -> p k c", c=128)
            for g8 in range(8):
                TR(pTs[0:32, g8, 0:32], wspsb[0:32, g8, :], ident[0:32, 0:32], ["wspsb", "ident"], [psk[7]])
            CP("dve", wspTs[0:32, :, :], pTs[0:32, :, 0:32], [psk[7]], ["wspTs"])

        unit([], s2_begin)

        def va_unit(ids):
            def ph1(i):
                n = 128 if i < 16 else 32
                pb0 = 2 * (i % 3)
                for h in range(2):
                    sl = slabs[ids[h]]
                    sk = f"slab{ids[h]}"
                    for k in range(8):
                        MM(ps[pb0 + h][0:n, :], hT[:, k, i * 128:i * 128 + n], sl[:, k, :], k == 0, k == 7,
                           ["hT", sk], [psk[pb0 + h]])
                v_ = va[i % 3]
                vk = f"va{i % 3}"
                c0 = 20 + 2 * i
                lk = f"ln_{i}"
                bst = bnst[i % 3]
                for h in range(2):
                    P.op("dve", lambda e, h=h, bst=bst, n=n, pb0=pb0: e.bn_stats(out=bst[0:n, 6 * h:6 * h + 6], in_=ps[pb0 + h][0:n, :]),
                         [psk[pb0 + h]], [lk])
                P.op("dve", lambda e, bst=bst, n=n, c0=c0: e.bn_aggr(out=ss[0:n, c0:c0 + 2], in_=bst[0:n, 0:12]), [lk], [lk])
                TS("dve", st2[0:n, c0:c0 + 1], ss[0:n, c0 + 1:c0 + 2], EPS, None, ALU.add, None, [lk], [lk])
                TT("pool", st2[0:n, c0:c0 + 1], st2[0:n, c0:c0 + 1], nhalf[0:n, 0:1], ALU.pow, [lk, "nhalf"], [lk])
                STT("dve", st2[0:n, c0 + 1:c0 + 2], ss[0:n, c0:c0 + 1], -1.0, st2[0:n, c0:c0 + 1], ALU.mult, ALU.mult, [lk], [lk])
                for h in range(2):
                    ACT(v_[0:n, 512 * h:512 * h + 512], ps[pb0 + h][0:n, :], AF.Identity, [psk[pb0 + h], lk], [vk],
                        scale=st2[0:n, c0:c0 + 1], bias=st2[0:n, c0 + 1:c0 + 2])

            def ph2(i):
                n = 128 if i < 16 else 32
                v_ = va[i % 3]
                vk = f"va{i % 3}"
                c0 = 20 + 2 * i
                lk = f"ln_{i}"
                TT("dve", v_[0:n, :], v_[0:n, :], lng[0:n, :], ALU.mult, [vk, "lng"], [vk])
                if i < 16:
                    TT("pool" if i % 2 == 0 else "dve", vn[0:n, i, :], v_[0:n, :], lnb[0:n, :], ALU.add, [vk, "lnb"], [f"vn{i}"])
                else:
                    TT("pool", v_[0:n, :], v_[0:n, :], lnb[0:n, :], ALU.add, [vk, "lnb"], [vk])
                    CP("dve", vn[0:n, i, :], v_[0:n, :], [vk], [f"vn{i}"])
                    LD(vch[:, :], v_[0:32, :], [], sem="vch", reads=[vk], final=True)

            for i in range(18):
                if i < 17:
                    ph1(i)
                if i >= 1:
                    ph2(i - 1)
                if i == 12:
                    wsp_prep()

        unit([[(0, w_in[:, 1024:1536])], [(0, w_in[:, 1536:2048])]], va_unit)

        def ya_unit(g8):
            def fn(ids):
                sl = slabs[ids[0]]
                sk = f"slab{ids[0]}"
                for ci, (c0, cn) in enumerate(CH):
                    b3 = 3 * (ci % 2)
                    U, Z, ZS = b3, b3 + 1, b3 + 2
                    for k in range(8):
                        MM(ps[U][:, 0:cn], sl[:, k, 0:128], hT[:, k, c0:c0 + cn], k == 0, k == 7, [sk, "hT"], [psk[U]])
                    for k in range(8):
                        MM(ps[Z][:, 0:cn], sl[:, k, 128:256], hT[:, k, c0:c0 + cn], k == 0, k == 7, [sk, "hT"], [psk[Z]])
                    if cn == 512:
                        for s_ in range(4):
                            i = c0 // 128 + s_
                            MM(ps[ZS][:, 128 * s_:128 * s_ + 128], vn[:, i, g8 * 128:(g8 + 1) * 128], wspT[:, g8, :],
                               True, True, ["vn", "wspT"], [psk[ZS]])
                        bias = bass.AP(tensor=hbsp.tensor, offset=hbsp[:, g8, :].offset, ap=[list(hbsp.ap[0]), [0, 4], [1, 128]])
                        zs_in = ps[ZS][:, :].rearrange("p (s c) -> p s c", c=128)
                    else:
                        MM(ps[ZS][:, 0:32], vn[0:32, 16, g8 * 128:(g8 + 1) * 128], wspTs[0:32, g8, :], True, True,
                           ["vn", "wspTs"], [psk[ZS]])
                        bias = bass.AP(tensor=hbsp.tensor, offset=hbsp[:, g8, :].offset, ap=[list(hbsp.ap[0]), [0, 4], [1, 8]])
                        zs_in = ps[ZS][:, 0:32].rearrange("p (s c) -> p s c", c=8)
                    tt_ = tmpA[ci % 2]
                    tk = [f"tA{ci % 2}{j}" for j in range(3)]
                    cs = cn // 4
                    ACT(tt_[0][:, 0:cn], ps[Z][:, 0:cn], AF.Tanh, [psk[Z]], [tk[0]], scale=0.5)
                    STT("dve", tt_[0][:, 0:cn], tt_[0][:, 0:cn], 1.0, ps[Z][:, 0:cn], ALU.add, ALU.mult, [tk[0], psk[Z]], [tk[0]])
                    STT("dve", tt_[1][:, 0:cn].rearrange("p (s c) -> p s c", c=cs), zs_in, 0.5, bias, ALU.mult, ALU.add,
                        [psk[ZS], "bsp"], [tk[1]])
                    TT("dve", tt_[1][:, 0:cn], tt_[1][:, 0:cn], ps[U][:, 0:cn], ALU.mult, [tk[1], psk[U]], [tk[1]])
                    TT("pool", yaT[:, g8, c0:c0 + cn], tt_[1][:, 0:cn], tt_[0][:, 0:cn], ALU.mult, [tk[1], tk[0]], ["yaT"])
            return fn

        for g8 in range(8):
            unit([[(0, w_in[:, 128 * g8:128 * g8 + 128]), (128, w_in[:, 2048 + 128 * g8:2048 + 128 * g8 + 128])]], ya_unit(g8))

        RX4 = Region(RXt, RXN, "RX4")
        mgT = RX4.alloc([8, T], BF16)
        tmpM = [[RX4.alloc([512], F32) for _ in range(6)] for _ in range(2)]
        xt4 = [RX4.alloc([D], F32) for _ in range(2)]
        ot4 = [RX4.alloc([D], F32) for _ in range(2)]
        sq4 = RX4.alloc([512], F32)
        for nm in (["mgT", "x40", "x41", "o40", "o41", "sq4"] + [f"tM{i}{j}" for i in range(2) for j in range(6)]):
            P.alias[nm] = "RX_ep"

        def s4_begin(ids):
            barrier(["RX"])

        unit([], s4_begin)

        def mg_unit(j):
            def fn(ids):
                sl = slabs[ids[0]]
                sk = f"slab{ids[0]}"
                for ci, (c0, cn) in enumerate(CH):
                    b4 = 4 * (ci % 2)
                    PA, PB, GA, GB = b4, b4 + 1, b4 + 2, b4 + 3
                    for k in range(8):
                        MM(ps[PA][:, 0:cn], sl[:, k, 0:128], yaT[:, k, c0:c0 + cn], k == 0, k == 7, [sk, "yaT"], [psk[PA]])
                    for k in range(4):
                        MM(ps[PB][:, 0:cn], sl[:, k, 128:256], ybT[:, k, c0:c0 + cn], k == 0, k == 3, [sk, "ybT"], [psk[PB]])
                    for k in range(8):
                        MM(ps[GA][:, 0:cn], sl[:, k, 256:384], hT[:, k, c0:c0 + cn], k == 0, k == 7, [sk, "hT"], [psk[GA]])
                    for k in range(8):
                        MM(ps[GB][:, 0:cn], sl[:, k, 384:512], hT[:, k, c0:c0 + cn], k == 0, k == 7, [sk, "hT"], [psk[GB]])
                    tt_ = tmpM[ci % 2]
                    tk = [f"tM{ci % 2}{q}" for q in range(6)]
                    ACT(tt_[0][:, 0:cn], ps[GA][:, 0:cn], AF.Tanh, [psk[GA]], [tk[0]], scale=0.5)
                    ACT(tt_[1][:, 0:cn], ps[GB][:, 0:cn], AF.Tanh, [psk[GB]], [tk[1]], scale=0.5)
                    TS("pool", tt_[2][:, 0:cn], tt_[0][:, 0:cn], 0.5, 0.5, ALU.mult, ALU.add, [tk[0]], [tk[2]])
                    TS("pool", tt_[3][:, 0:cn], tt_[1][:, 0:cn], 0.5, 0.5, ALU.mult, ALU.add, [tk[1]], [tk[3]])
                    TT("dve", tt_[4][:, 0:cn], ps[PA][:, 0:cn], tt_[2][:, 0:cn], ALU.mult, [psk[PA], tk[2]], [tk[4]])
                    TT("dve", tt_[5][:, 0:cn], ps[PB][:, 0:cn], tt_[3][:, 0:cn], ALU.mult, [psk[PB], tk[3]], [tk[5]])
                    TT("pool", mgT[:, j, c0:c0 + cn], tt_[4][:, 0:cn], tt_[5][:, 0:cn], ALU.add, [tk[4], tk[5]], ["mgT"])
            return fn

        for j in range(8):
            unit([[(0, w_pa[:, 128 * j:128 * j + 128]), (128, w_pb[:, 128 * j:128 * j + 128]),
                   (256, w_in[:, 8192 + 128 * j:8192 + 128 * j + 128]),
                   (384, w_in[:, 9216 + 128 * j:9216 + 128 * j + 128])]], mg_unit(j))

        def out_unit(ids):
            def ph1(i):
                n = 128 if i < 16 else 32
                pb0 = 2 * (i % 3)
                xk = f"x4{i % 2}"
                src = xp[i * 128:(i + 1) * 128, :] if i < 16 else xs[:, :]
                LD(xt4[i % 2][0:n, :], src, [xk], sem=xk)
                for h in range(2):
                    sl = slabs[ids[h]]
                    sk = f"slab{ids[h]}"
                    for k in range(8):
                        MM(ps[pb0 + h][0:n, :], mgT[:, k, i * 128:i * 128 + n], sl[:, k, :], k == 0, k == 7,
                           ["mgT", sk], [psk[pb0 + h]])
                c0 = 64 + 2 * i
                fk = f"fst_{i}"
                for h in range(2):
                    ACT(sq4[0:n, :], ps[pb0 + h][0:n, :], AF.Square, [psk[pb0 + h]], ["sq4", fk], accum=ss[0:n, c0 + h:c0 + h + 1])
                TT("dve", st2[0:n, c0:c0 + 1], ss[0:n, c0:c0 + 1], ss[0:n, c0 + 1:c0 + 2], ALU.add, [fk], [fk])
                TS("dve", st2[0:n, c0:c0 + 1], st2[0:n, c0:c0 + 1], 1.0 / D, EPS, ALU.mult, ALU.add, [fk], [fk])
                TT("pool", st2[0:n, c0:c0 + 1], st2[0:n, c0:c0 + 1], nhalf[0:n, 0:1], ALU.pow, [fk, "nhalf"], [fk])

            def ph2(i):
                n = 128 if i < 16 else 32
                pb0 = 2 * (i % 3)
                xk, ok = f"x4{i % 2}", f"o4{i % 2}"
                c0 = 64 + 2 * i
                fk = f"fst_{i}"
                G_ = GGp if i < 16 else GGs
                gk = "GGp" if i < 16 else "GGs"
                o_ = ot4[i % 2]
                for h in range(2):
                    cs = slice(512 * h, 512 * h + 512)
                    STT("dve", o_[0:n, cs], ps[pb0 + h][0:n, :], st2[0:n, c0:c0 + 1], G_[0:n, cs], ALU.mult, ALU.mult,
                        [psk[pb0 + h], fk, gk], [ok])
                TT("pool" if i % 2 == 0 else "dve", o_[0:n, :], o_[0:n, :], xt4[i % 2][0:n, :], ALU.add, [ok, xk], [ok])
                dst = yp[i * 128:(i + 1) * 128, :] if i < 16 else ys[:, :]
                LD(dst, o_[0:n, :], [], sem=ok, reads=[ok], final=True)

            for i in range(18):
                if i < 17:
                    ph1(i)
                if i >= 1:
                    ph2(i - 1)

        unit([[(0, w_out[:, 0:512])], [(0, w_out[:, 512:1024])]], out_unit)

        run_units()
        P.emit()
    return nc


_CACHE = {}


def kernel(x_prompt, x_sample, cache_kv_w128, cache_kv_w512, cache_kv_w2048, c_prompt, c_sample,
           w_cond, b_cond, g_pre, w_in, ln_v_g, ln_v_b, w_spatial, b_spatial,
           w_proj_a, w_proj_b, w_out, g_post):
    f = lambda a: np.ascontiguousarray(np.asarray(a, dtype=np.float32))
    if "nc" not in _CACHE:
        _CACHE["nc"] = build_nc()
    nc = _CACHE["nc"]
    x_prompt, x_sample = f(x_prompt), f(x_sample)
    caches = [f(cache_kv_w128), f(cache_kv_w512), f(cache_kv_w2048)]
    c_prompt, c_sample = f(c_prompt), f(c_sample)
    shared = {
        "w_cond": f(w_cond)[0], "b_cond": f(b_cond)[0], "g_pre": f(g_pre)[0], "w_in": f(w_in)[0],
        "ln_v_g": f(ln_v_g)[0], "ln_v_b": f(ln_v_b)[0], "w_spatial": f(w_spatial)[0], "b_spatial": f(b_spatial)[0],
        "w_proj_a": f(w_proj_a)[0], "w_proj_b": f(w_proj_b)[0], "w_out": f(w_out)[0], "g_post": f(g_post)[0],
    }
    in_maps = []
    for i in range(8):
        m = dict(shared)
        m["xp"] = x_prompt[i]
        m["xs"] = x_sample[4 * i:4 * i + 4].reshape(NS, D)
        m["cc"] = np.concatenate([c_prompt[i:i + 1], c_sample[4 * i:4 * i + 4]], axis=0)
        for g in range(3):
            c = caches[g][0, 4 * i:4 * i + 4]
            m[f"ck{g}"] = np.ascontiguousarray(c.reshape(4, c.shape[1], 2, 512))
        in_maps.append(m)
    res = run_bass_kernel_spmd(nc, in_maps, core_ids=list(range(8)))
    R = res.results
    y_p = np.stack([R[i]["yp"] for i in range(8)], axis=0)
    y_s = np.concatenate([R[i]["ys"].reshape(4, 8, D) for i in range(8)], axis=0)
    outs = [y_p, y_s]
    for g in range(3):
        win = PAIRS[g][0]
        outs.append(np.stack([R[i][f"kvp{g}"].reshape(win, 2, 8, 64) for i in range(8)], axis=0)[None])
    for g in range(3):
        outs.append(np.concatenate([R[i][f"kvs{g}"].reshape(4, 8, 2, 8, 64) for i in range(8)], axis=0)[None])
    outs.append(np.concatenate([R[i]["vch"].reshape(4, 8, D) for i in range(8)], axis=0)[None])
    return tuple(np.ascontiguousarray(o.astype(np.float32)) for o in outs)
```
